# Optimizing a Trainium2 kernel written in Bass

```python
import math
import jax, jax.numpy as jnp
from jax import lax
import numpy as np

D_MODEL = 2048
BATCH = 8
SEQ = 4096
DEPTH = 4

GRID_W = 64
CTX_LEN = 256
HEAD_DIM = 128
H_A = 4
WIN_H = 8
WIN_W = 16
H_B = 8
KV_B = 2
H_C = 4
DC = HEAD_DIM // 2
Q_BLOCK = 128
ROPE_THETA = 10000.0
EPS = 1e-6
A_W = H_A * HEAD_DIM
B_QW = H_B * HEAD_DIM
B_KW = KV_B * HEAD_DIM
C_W = H_C * HEAD_DIM
GATE_W = 3 * D_MODEL
SPLITS = (A_W, A_W, A_W, A_W, B_QW, B_KW, B_KW, B_QW, C_W, C_W, C_W, C_W, GATE_W)
N_IN = sum(SPLITS)

kernel_name = "hybrid_natten_gqa_diffattn_prefix_dit"


def _rmsnorm(x, w):
    xf = x.astype(jnp.float32)
    y = xf * lax.rsqrt(jnp.mean(xf * xf, axis=-1, keepdims=True) + EPS)
    return y.astype(x.dtype) * w


def _axial_rope_tables(length, rot_dim):
    pos = jnp.arange(length)
    row = (pos // GRID_W).astype(jnp.float32)
    col = (pos % GRID_W).astype(jnp.float32)
    n = rot_dim // 4
    inv_freq = ROPE_THETA ** (-jnp.arange(n, dtype=jnp.float32) / n)
    ang = jnp.concatenate([row[:, None] * inv_freq, col[:, None] * inv_freq], axis=-1)
    return jnp.cos(ang), jnp.sin(ang)


def _apply_rope(x, cos, sin):
    half = x.shape[-1] // 2
    shape = (cos.shape[0],) + (1,) * (x.ndim - 3) + (half,)
    co = cos.reshape(shape)
    si = sin.reshape(shape)
    xf = x.astype(jnp.float32)
    x1, x2 = xf[..., :half], xf[..., half:]
    return jnp.concatenate([x1 * co - x2 * si, x1 * si + x2 * co], axis=-1).astype(x.dtype)


def _split_cols(p):
    idx = []
    acc = 0
    for s in SPLITS[:-1]:
        acc += s
        idx.append(acc)
    return jnp.split(p, idx, axis=-1)


def _prep(p, q_norm_w, k_norm_w, rope_b, rope_c):
    b, length = p.shape[0], p.shape[1]
    aq, ak, av, az, bq, bk, bv, bz, cq, ck, cv, cz, gates = _split_cols(p)
    aq = aq.reshape(b, length, H_A, HEAD_DIM)
    ak = ak.reshape(b, length, H_A, HEAD_DIM)
    av = av.reshape(b, length, H_A, HEAD_DIM)
    bq = _rmsnorm(bq.reshape(b, length, H_B, HEAD_DIM), q_norm_w)
    bk = _rmsnorm(bk.reshape(b, length, KV_B, HEAD_DIM), k_norm_w)
    bv = bv.reshape(b, length, KV_B, HEAD_DIM)
    cq = cq.reshape(b, length, H_C, 2, DC)
    ck = ck.reshape(b, length, H_C, 2, DC)
    cv = cv.reshape(b, length, H_C, HEAD_DIM)
    if rope_b is not None:
        bq = _apply_rope(bq, *rope_b)
        bk = _apply_rope(bk, *rope_b)
        cq = _apply_rope(cq, *rope_c)
        ck = _apply_rope(ck, *rope_c)
    return aq, ak, av, az, bq, bk, bv, bz, cq, ck, cv, cz, gates


def _blocked_map(fn, q):
    b, lq = q.shape[0], q.shape[1]
    nb = lq // Q_BLOCK
    qb = jnp.moveaxis(q.reshape((b, nb, Q_BLOCK) + q.shape[2:]), 1, 0)
    o = lax.map(fn, qb)
    return jnp.moveaxis(o, 0, 1).reshape((b, lq) + o.shape[3:])


def _gqa_attend(q, k, v):
    scale = q.shape[-1] ** -0.5

    def block(qi):
        s = jnp.einsum("bqgrd,bkgd->bgrqk", qi, k).astype(jnp.float32) * scale
        p = jax.nn.softmax(s, axis=-1).astype(v.dtype)
        return jnp.einsum("bgrqk,bkgd->bqgrd", p, v)

    o = _blocked_map(block, q)
    return o.reshape(o.shape[0], o.shape[1], -1)


def _diff_attend(q, k, v, lam, subln_w, lam_init):
    scale = q.shape[-1] ** -0.5

    def block(qi):
        s = jnp.einsum("bqhmd,bkhmd->bhmqk", qi, k).astype(jnp.float32) * scale
        p = jax.nn.softmax(s, axis=-1)
        a = (p[:, :, 0] - lam * p[:, :, 1]).astype(v.dtype)
        return jnp.einsum("bhqk,bkhd->bqhd", a, v)

    o = _blocked_map(block, q)
    o = _rmsnorm(o, subln_w) * (1.0 - lam_init)
    return o.reshape(o.shape[0], o.shape[1], -1)


def _neighbourhood_attend(q, k, v, k_ctx, v_ctx, rpb):
    b, length, h, d = q.shape
    rows = length // GRID_W
    kh = min(WIN_H, rows)
    scale = d ** -0.5
    r = jnp.arange(rows)
    row_start = jnp.clip(r - kh // 2, 0, rows - kh)
    band = row_start[:, None] + jnp.arange(kh)
    cidx = jnp.arange(GRID_W)
    col_start = jnp.clip(cidx - WIN_W // 2, 0, GRID_W - WIN_W)
    in_win = (cidx[None, :] >= col_start[:, None]) & (cidx[None, :] < col_start[:, None] + WIN_W)
    dr = band - r[:, None] + (WIN_H - 1)
    dc = jnp.clip(cidx[None, :] - cidx[:, None], -(WIN_W - 1), WIN_W - 1) + (WIN_W - 1)
    bias = rpb[:, dr[:, :, None, None], dc[None, None]]
    bias = jnp.transpose(bias, (0, 1, 3, 2, 4))
    qg = q.reshape(b, rows, GRID_W, h, d)
    kb = k.reshape(b, rows, GRID_W, h, d)[:, band]
    vb = v.reshape(b, rows, GRID_W, h, d)[:, band]
    s_win = jnp.einsum("brqhd,brjkhd->bhrqjk", qg, kb).astype(jnp.float32) * scale + bias
    s_win = jnp.where(in_win[:, None, :], s_win, -jnp.inf)
    s_ctx = jnp.einsum("brqhd,bchd->bhrqc", qg, k_ctx).astype(jnp.float32) * scale
    nwin = kh * GRID_W
    s = jnp.concatenate([s_win.reshape(b, h, rows, GRID_W, nwin), s_ctx], axis=-1)
    p = jax.nn.softmax(s, axis=-1).astype(v.dtype)
    p_win = p[..., :nwin].reshape(b, h, rows, GRID_W, kh, GRID_W)
    p_ctx = p[..., nwin:]
    o = jnp.einsum("bhrqjk,brjkhd->brqhd", p_win, vb) + jnp.einsum("bhrqc,bchd->brqhd", p_ctx, v_ctx)
    return o.reshape(b, length, h * d)


def _gated_merge(o_a, z_a, o_b, z_b, o_c, z_c, gates, b_gate, w_bo_a, w_bo_b, w_bo_c, w_out):
    g_a, g_b, g_c = jnp.split(jax.nn.sigmoid(gates + b_gate), 3, axis=-1)
    y = (g_a * ((o_a * jax.nn.silu(z_a)) @ w_bo_a)
         + g_b * ((o_b * jax.nn.silu(z_b)) @ w_bo_b)
         + g_c * ((o_c * jax.nn.silu(z_c)) @ w_bo_c))
    return y @ w_out


def _hybrid_layer(x, xc, c, c_ctx, layer_idx, rope_b, rope_c, update_ctx, norm_w, w_ada, b_ada, w_in, b_gate,
                  rpb, q_norm_w, k_norm_w, lam_q1, lam_k1, lam_q2, lam_k2, subln_w, w_bo_a, w_bo_b, w_bo_c, w_out):
    b, length = x.shape[0], x.shape[1]
    shift, scale, gate = jnp.split(jax.nn.silu(c) @ w_ada + b_ada, 3, axis=-1)
    shift_c, scale_c, gate_c = jnp.split(jax.nn.silu(c_ctx) @ w_ada + b_ada, 3, axis=-1)
    h = _rmsnorm(x, norm_w) * (1.0 + scale[:, None]) + shift[:, None]
    hc = _rmsnorm(xc, norm_w) * (1.0 + scale_c) + shift_c
    aq, ak, av, az, bq, bk, bv, bz, cq, ck, cv, cz, gl = _prep(h @ w_in, q_norm_w, k_norm_w, rope_b, rope_c)
    aqc, akc, avc, azc, bqc, bkc, bvc, bzc, cqc, ckc, cvc, czc, glc = _prep(hc @ w_in, q_norm_w, k_norm_w, None, None)
    lam_init = 0.8 - 0.6 * math.exp(-0.3 * layer_idx)
    lam = (jnp.exp(jnp.sum(lam_q1.astype(jnp.float32) * lam_k1.astype(jnp.float32)))
           - jnp.exp(jnp.sum(lam_q2.astype(jnp.float32) * lam_k2.astype(jnp.float32))) + lam_init)
    rep = H_B // KV_B
    o_a = _neighbourhood_attend(aq, ak, av, akc, avc, rpb)
    o_b = _gqa_attend(bq.reshape(b, length, KV_B, rep, HEAD_DIM),
                      jnp.concatenate([bkc, bk], axis=1), jnp.concatenate([bvc, bv], axis=1))
    o_c = _diff_attend(cq, jnp.concatenate([ckc, ck], axis=1), jnp.concatenate([cvc, cv], axis=1),
                       lam, subln_w, lam_init)
    out = _gated_merge(o_a, az, o_b, bz, o_c, cz, gl, b_gate, w_bo_a, w_bo_b, w_bo_c, w_out)
    x = x + gate[:, None] * out
    if update_ctx:
        lc = xc.shape[1]
        oc_a = _gqa_attend(aqc[:, :, :, None], akc, avc)
        oc_b = _gqa_attend(bqc.reshape(b, lc, KV_B, rep, HEAD_DIM), bkc, bvc)
        oc_c = _diff_attend(cqc, ckc, cvc, lam, subln_w, lam_init)
        out_c = _gated_merge(oc_a, azc, oc_b, bzc, oc_c, czc, glc, b_gate, w_bo_a, w_bo_b, w_bo_c, w_out)
        xc = xc + gate_c * out_c
    return x, xc


def setup_inputs(seed: int = 0) -> dict:
    key = jax.random.key(seed)
    ks = jax.random.split(key, 24)
    f32 = jnp.float32
    sd = D_MODEL ** -0.5

    def nrm(k, shape, s):
        return jax.random.normal(k, shape, f32) * s

    return {
        "x": nrm(ks[0], (BATCH, SEQ, D_MODEL), 1.0),
        "c": nrm(ks[1], (BATCH, D_MODEL), 1.0),
        "ctx": nrm(ks[2], (BATCH, CTX_LEN, D_MODEL), 1.0),
        "c_ctx": nrm(ks[3], (D_MODEL,), 1.0),
        "norm_w": 1.0 + nrm(ks[4], (DEPTH, D_MODEL), 0.02),
        "w_ada": nrm(ks[5], (DEPTH, D_MODEL, 3 * D_MODEL), 0.5 * sd),
        "b_ada": nrm(ks[6], (DEPTH, 3 * D_MODEL), 0.01),
        "w_in": nrm(ks[7], (DEPTH, D_MODEL, N_IN), sd),
        "b_gate": nrm(ks[8], (DEPTH, GATE_W), 0.01),
        "rpb": nrm(ks[9], (DEPTH, H_A, 2 * WIN_H - 1, 2 * WIN_W - 1), 0.02),
        "q_norm_w": 1.0 + nrm(ks[10], (DEPTH, HEAD_DIM), 0.02),
        "k_norm_w": 1.0 + nrm(ks[11], (DEPTH, HEAD_DIM), 0.02),
        "lam_q1": nrm(ks[12], (DEPTH, DC), 0.1),
        "lam_k1": nrm(ks[13], (DEPTH, DC), 0.1),
        "lam_q2": nrm(ks[14], (DEPTH, DC), 0.1),
        "lam_k2": nrm(ks[15], (DEPTH, DC), 0.1),
        "subln_w": 1.0 + nrm(ks[16], (DEPTH, HEAD_DIM), 0.02),
        "w_bo_a": nrm(ks[17], (DEPTH, A_W, D_MODEL), A_W ** -0.5),
        "w_bo_b": nrm(ks[18], (DEPTH, B_QW, D_MODEL), B_QW ** -0.5),
        "w_bo_c": nrm(ks[19], (DEPTH, C_W, D_MODEL), C_W ** -0.5),
        "w_out": nrm(ks[20], (DEPTH, D_MODEL, D_MODEL), sd),
        "final_norm_w": 1.0 + nrm(ks[21], (D_MODEL,), 0.02),
    }


def reference(x, c, ctx, c_ctx, norm_w, w_ada, b_ada, w_in, b_gate, rpb, q_norm_w, k_norm_w,
              lam_q1, lam_k1, lam_q2, lam_k2, subln_w, w_bo_a, w_bo_b, w_bo_c, w_out, final_norm_w):
    length = x.shape[1]
    rope_b = _axial_rope_tables(length, HEAD_DIM)
    rope_c = _axial_rope_tables(length, DC)
    xl, xc = x, ctx
    for l in range(DEPTH):
        xl, xc = _hybrid_layer(xl, xc, c, c_ctx, l, rope_b, rope_c, l < DEPTH - 1,
                               norm_w[l], w_ada[l], b_ada[l], w_in[l], b_gate[l], rpb[l],
                               q_norm_w[l], k_norm_w[l], lam_q1[l], lam_k1[l], lam_q2[l], lam_k2[l],
                               subln_w[l], w_bo_a[l], w_bo_b[l], w_bo_c[l], w_out[l])
    return _rmsnorm(xl, final_norm_w)
```

```python
import math
from contextlib import ExitStack

import numpy as np
import concourse.bass as bass
import concourse.mybir as mybir
from concourse.bass_utils import run_bass_kernel_spmd

F32 = mybir.dt.float32
BF16 = mybir.dt.bfloat16
U8 = mybir.dt.uint8
AF = mybir.ActivationFunctionType
ALU = mybir.AluOpType
AX = mybir.AxisListType

P = 128
D = 2048
KC = 16
LC = 256
L = 4096
T = LC + L
NT = T // P
GRID_W = 64
HD = 128
N_IN = 12800
EPS = 1e-6
DEPTH = 4
NB_A = 21
MT = [(0, 256)] + [(256 + 512 * i, 512) for i in range(8)]

ENGS = ("pe", "act", "dve", "pool", "sp")


class Buf:
    __slots__ = ("name", "w", "r", "excl")

    def __init__(self, name, fence=(), excl=False):
        self.name = name
        self.w = None
        self.r = list(fence)
        self.excl = excl


class Ev:
    __slots__ = ("eng", "idx", "dsem", "dval")

    def __init__(self, eng, idx, dsem=None, dval=None):
        self.eng, self.idx, self.dsem, self.dval = eng, idx, dsem, dval


def _dedupe(evs):
    best = {}
    for d in evs:
        k = ("d",) + d.dsem if d.dsem is not None else ("p", d.eng)
        v = d.dval if d.dsem is not None else d.idx
        if k not in best or v > best[k][0]:
            best[k] = (v, d)
    return [x[1] for x in best.values()]


class Sched:
    def __init__(self, nc, n_dma_sems=10):
        self.nc = nc
        self.ops = {e: [] for e in ENGS}
        self.n_dma_sems = n_dma_sems
        self.dma_rr = {e: 0 for e in ENGS}
        self.dma_cnt = {}
        self.dma_last = {}
        self.fence = []
        self.dead = False
        self.dead_tail = None

    def kill(self):
        if not self.dead:
            self.dead_tail = self.tail_events()
            self.dead = True

    def _deps(self, reads, writes):
        deps = []
        raw = set()
        for b in reads:
            if b.w is not None:
                deps.append(b.w)
                raw.add(id(b.w))
            if b.excl:
                deps.extend(b.r)
        for b in writes:
            if b.w is not None:
                deps.append(b.w)
            deps.extend(b.r)
        self._raw = raw
        return deps


    def op(self, eng, fn, reads=(), writes=()):
        if self.dead:
            return None
        deps = self._deps(reads, writes)
        ev = Ev(eng, len(self.ops[eng]))
        self.ops[eng].append((deps, fn, ev, False, self._raw))
        for b in reads:
            b.r = [x for x in b.r if not (x.dsem is None and x.eng == eng)]
            b.r.append(ev)
        for b in writes:
            b.w = ev
            b.r = []
        return ev

    def dma(self, eng, fn, reads=(), writes=()):
        if self.dead:
            return None
        deps = self._deps(reads, writes)
        slot = self.dma_rr[eng] % self.n_dma_sems
        self.dma_rr[eng] += 1
        key = (eng, slot)
        cnt = self.dma_cnt.get(key, 0) + 1
        self.dma_cnt[key] = cnt
        prev = self.dma_last.get(key)
        if prev is not None:
            deps.append(prev)
        ev = Ev(eng, len(self.ops[eng]), dsem=key, dval=16 * cnt)
        self.dma_last[key] = ev
        self.ops[eng].append((deps, fn, ev, True, self._raw))
        for b in reads:
            b.r = _dedupe(b.r + [ev])
        for b in writes:
            b.w = ev
            b.r = []
        return ev

    def tail_events(self):
        evs = []
        for e in ENGS:
            for deps, fn, ev, is_dma, raw in reversed(self.ops[e]):
                if not is_dma:
                    evs.append(ev)
                    break
        evs.extend(self.dma_last.values())
        return evs

    def finalize(self, final_waits=()):
        nc = self.nc
        needed = {e: set() for e in ENGS}
        def same_needed(e, d, is_dma, raw):
            if e == "pe" or e == "sp":
                return False
            return is_dma or (id(d) in raw)

        for e in ENGS:
            for deps, fn, ev, is_dma, raw in self.ops[e]:
                for d in deps:
                    if d.dsem is None and (d.eng != e or same_needed(e, d, is_dma, raw)):
                        needed[d.eng].add(d.idx)
        for d in final_waits:
            if d.dsem is None:
                needed[d.eng].add(d.idx)
        val = {e: {} for e in ENGS}
        for e in ENGS:
            for c, i in enumerate(sorted(needed[e])):
                val[e][i] = c + 1
        with ExitStack() as stack:
            psem = {e: stack.enter_context(nc.semaphore("prog_" + e)) for e in ENGS}
            dsem = {key: stack.enter_context(nc.semaphore("dma_%s_%d" % key)) for key in self.dma_cnt}
            block = stack.enter_context(nc.Block())

            def make_body(e, extra_final):
                def body(engine):
                    seen = {}
                    for deps, fn, ev, is_dma, raw in self.ops[e]:
                        want = {}
                        for d in deps:
                            if d.dsem is not None:
                                k, v = ("d",) + d.dsem, d.dval
                            else:
                                if d.eng == e and not same_needed(e, d, is_dma, raw):
                                    continue
                                k, v = ("p", d.eng), val[d.eng][d.idx]
                            if seen.get(k, 0) >= v:
                                continue
                            if want.get(k, 0) < v:
                                want[k] = v
                        for k, v in want.items():
                            sem = dsem[k[1:]] if k[0] == "d" else psem[k[1]]
                            engine.wait_ge(sem, v)
                            seen[k] = v
                        ins = fn(engine)
                        if is_dma:
                            ins.then_inc(dsem[ev.dsem], 16)
                        elif ev.idx in val[e]:
                            ins.then_inc(psem[e], 1)
                    for d in extra_final:
                        if d.dsem is not None:
                            engine.wait_ge(dsem[d.dsem], d.dval)
                        else:
                            engine.wait_ge(psem[d.eng], val[d.eng][d.idx])
                return body

            block.tensor(make_body("pe", ()))
            block.scalar(make_body("act", ()))
            block.vector(make_body("dve", ()))
            block.gpsimd(make_body("pool", ()))
            block.sync(make_body("sp", tuple(final_waits)))
        return {e: len(self.ops[e]) for e in ENGS}


_DTSZ = {F32: 4, BF16: 2, U8: 1}


class Arena:
    def __init__(self, S, tensor, size):
        self.S, self.t, self.size = S, tensor, size
        self.off = 0
        self.live = []

    def alloc(self, name, free_shape, dtype):
        n = 1
        for s in free_shape:
            n *= s
        nbytes = n * _DTSZ[dtype]
        off = (self.off + 63) // 64 * 64
        assert off + nbytes <= self.size, "SBUF arena overflow at %s: need %d have %d" % (name, off + nbytes, self.size)
        ap = self.t[:, off:off + nbytes]
        if dtype != U8:
            ap = ap.bitcast(dtype)
        if len(free_shape) == 2:
            ap = ap.rearrange("p (a b) -> p a b", a=free_shape[0])
        elif len(free_shape) == 3:
            ap = ap.rearrange("p (a b c) -> p a b c", a=free_shape[0], b=free_shape[1])
        self.off = off + nbytes
        b = Buf(name, self.S.fence)
        self.live.append((self.off, b))
        return ap, b

    def extra_buf(self, name):
        b = Buf(name, self.S.fence)
        self.live.append((self.off, b))
        return b

    def mark(self):
        return self.off

    def release(self, mark):
        evs = list(self.S.fence)
        keep = []
        for off, b in self.live:
            if off > mark:
                if b.w is not None:
                    evs.append(b.w)
                evs.extend(b.r)
            else:
                keep.append((off, b))
        self.live = keep
        self.S.fence = _dedupe(evs)
        self.off = mark


class DramT:
    def __init__(self, ap, name, tok_axis):
        self.ap = ap
        self.tok_axis = tok_axis
        self.b = [Buf("%s_%d" % (name, i)) for i in range(len(MT))]

    def bufs(self, tok0, n):
        return [self.b[i] for i, (t0, tn) in enumerate(MT) if t0 < tok0 + n and tok0 < t0 + tn]


def _rope_tables(rot_dim):
    pos = np.arange(L)
    row = (pos // GRID_W).astype(np.float32)
    col = (pos % GRID_W).astype(np.float32)
    n = rot_dim // 4
    inv_freq = (10000.0 ** (-np.arange(n, dtype=np.float32) / n)).astype(np.float32)
    ang = np.concatenate([row[:, None] * inv_freq, col[:, None] * inv_freq], axis=-1).astype(np.float32)
    cos, sin = np.cos(ang).astype(np.float32), np.sin(ang).astype(np.float32)
    f = lambda a: np.ascontiguousarray(a.reshape(L // P, P, -1).transpose(1, 0, 2))
    return f(cos), f(sin)


def _natten_plan():
    rows = L // GRID_W
    kh, win_w = 8, 16
    r = np.arange(rows)
    row_start = np.clip(r - kh // 2, 0, rows - kh)
    cidx = np.arange(GRID_W)
    col_start = np.clip(cidx - win_w // 2, 0, GRID_W - win_w)
    plan = []
    case_base = {}
    tile_defs = []
    for j in range(rows // 2):
        qrows = [2 * j, 2 * j + 1]
        krows = sorted(set(int(x) for qr in qrows for x in range(row_start[qr], row_start[qr] + kh)))
        jjs = sorted(set(kr // 2 for kr in krows))
        case = j if j in (0, 1, rows // 2 - 2, rows // 2 - 1) else "mid"
        new_case = case not in case_base
        if new_case:
            case_base[case] = len(tile_defs)
        ent = []
        for wi, jj in enumerate(jjs):
            kr = np.repeat(np.array([2 * jj, 2 * jj + 1]), GRID_W)
            kc = np.tile(cidx, 2)
            qr = np.repeat(np.array(qrows), GRID_W)
            qc = np.tile(cidx, 2)
            in_rows = (kr[:, None] >= row_start[qr][None, :]) & (kr[:, None] < row_start[qr][None, :] + kh)
            in_cols = (kc[:, None] >= col_start[qc][None, :]) & (kc[:, None] < col_start[qc][None, :] + win_w)
            mask = in_rows & in_cols
            dr = np.clip(kr[:, None] - qr[None, :] + 7, 0, 14)
            dc = np.clip(kc[:, None] - qc[None, :], -15, 15) + 15
            if new_case:
                tile_defs.append((dr, dc, mask))
            else:
                d0, d1, m0 = tile_defs[case_base[case] + wi]
                assert (m0 == mask).all() and ((d0 * m0) == (dr * mask)).all() and ((d1 * m0) == (dc * mask)).all()
            ent.append((jj, case_base[case] + wi))
        plan.append(ent)
    return plan, tile_defs


_NPLAN, _NTILES = _natten_plan()
assert len(_NTILES) == NB_A, len(_NTILES)


def _bias_tables(rpb):
    nl = rpb.shape[0]
    out = np.empty((nl, P, 4, NB_A, P), np.float32)
    for i, (dr, dc, mask) in enumerate(_NTILES):
        g = rpb[:, :, dr, dc]
        g = np.where(mask[None, None], g, np.float32(-30000.0))
        out[:, :, :, i, :] = g.transpose(0, 2, 1, 3)
    return out


class _Stop(Exception):
    pass


def build_program(nl, first_layer, is_final, dbg=False, stop=None):
    nc = bass.Bass("TRN2", target_bir_lowering=False)
    dram_in = lambda name, shape, dt=F32: nc.dram_tensor(name, list(shape), dt, kind="ExternalInput").ap()
    dram_out = lambda name, shape, dt=F32: nc.dram_tensor(name, list(shape), dt, kind="ExternalOutput").ap()

    def dram_scr(name, shape, dt):
        if dbg:
            return nc.dram_tensor(name, list(shape), dt, kind="ExternalOutput").ap()
        return nc.dram_tensor(name, list(shape), dt).ap()

    xin = dram_in("xin", [T, D])
    ccT = dram_in("ccT", [P, KC, 2])
    norm_wT = dram_in("norm_wT", [nl, P, KC])
    w_ada = dram_in("w_ada", [nl, D, 3 * D])
    b_adaT = dram_in("b_adaT", [nl, P, 48])
    w_in = dram_in("w_in", [nl, D, N_IN])
    b_gateT = dram_in("b_gateT", [nl, P, 48])
    biasA = dram_in("biasA", [nl, P, 4, NB_A, P])
    qnw = dram_in("qnw", [nl, P, HD])
    knw = dram_in("knw", [nl, P, HD])
    lamv = dram_in("lamv", [nl, P, 4, 64])
    sublnT = dram_in("sublnT", [nl, P, 1])
    w_bo = dram_in("w_bo", [nl, D, D])
    w_out = dram_in("w_out", [nl, D, D])
    fnw = dram_in("fnw", [P, D])
    cosb_d = dram_in("cosb", [P, 32, 64])
    sinb_d = dram_in("sinb", [P, 32, 64])
    cosc_d = dram_in("cosc", [P, 32, 32])
    sinc_d = dram_in("sinc", [P, 32, 32])
    ident_d = dram_in("ident", [P, P])
    if is_final:
        yout = dram_out("y", [L, D])
    else:
        xout = dram_out("xout", [T, D])

    Xs = dram_scr("Xs", [T, D], F32)
    QAT = DramT(dram_scr("QAT", [512, T], BF16), "QAT", 1)
    KAT = DramT(dram_scr("KAT", [512, T], BF16), "KAT", 1)
    VA = DramT(dram_scr("VA", [T, 512], BF16), "VA", 0)
    SZT = DramT(dram_scr("SZT", [2048, T], BF16), "SZT", 1)
    QBT = DramT(dram_scr("QBT", [1024, T], BF16), "QBT", 1)
    KBT = DramT(dram_scr("KBT", [256, T], BF16), "KBT", 1)
    VB = DramT(dram_scr("VB", [T, 256], BF16), "VB", 0)
    QCT = DramT(dram_scr("QCT", [512, T], BF16), "QCT", 1)
    KCT = DramT(dram_scr("KCT", [512, T], BF16), "KCT", 1)
    VC = DramT(dram_scr("VC", [T, 512], BF16), "VC", 0)
    GT = DramT(dram_scr("GT", [6144, T], BF16), "GT", 1)
    UT = DramT(dram_scr("UT", [2048, T], BF16), "UT", 1)
    xs_b = [Buf("Xs_%d" % i) for i in range(NT)]
    xo_b = [Buf("xo_%d" % i) for i in range(NT)]
    yo_b = [Buf("y_%d" % i) for i in range(NT)]

    ARENA = 196 * 1024
    with ExitStack() as st:
        arena_t = st.enter_context(nc.sbuf_tensor("arena", [P, ARENA], U8))
        small_t = st.enter_context(nc.sbuf_tensor("small", [P, 8 * 1024], U8))
        ps = [st.enter_context(nc.psum_tensor("ps%d" % i, [P, 512], F32)) for i in range(8)]
        pb = [Buf("ps%d" % i, excl=True) for i in range(8)]
        S = Sched(nc)
        A = Arena(S, arena_t, ARENA)
        SM = Arena(S, small_t, 8 * 1024)

        def mm(out, lhsT, rhs, start, stop, R, W):
            return S.op("pe", lambda e: e.matmul(out, lhsT, rhs, start=start, stop=stop), R, W)

        def tr(out, in_, ident, R, W):
            return S.op("pe", lambda e: e.transpose(out, in_, ident), R, W)

        def act(out, in_, func, R, W, bias=None, scale=None, accum=None):
            kw = {}
            if bias is not None:
                kw["bias"] = bias
            if scale is not None:
                kw["scale"] = scale
            if accum is not None:
                kw["accum_out"] = accum
            return S.op("act", lambda e: e.activation(out=out, in_=in_, func=func, **kw), R, W)

        def tt(eng, out, a, b, op, R, W):
            return S.op(eng, lambda e: e.tensor_tensor(out=out, in0=a, in1=b, op=op), R, W)

        def ts(eng, out, a, s1, op0, R, W, s2=None, op1=None):
            if op1 is None:
                return S.op(eng, lambda e: e.tensor_scalar(out=out, in0=a, scalar1=s1, scalar2=None, op0=op0), R, W)
            return S.op(eng, lambda e: e.tensor_scalar(out=out, in0=a, scalar1=s1, scalar2=s2, op0=op0, op1=op1), R, W)

        def stt(out, a, scalar, b, op0, op1, R, W):
            return S.op("dve", lambda e: e.scalar_tensor_tensor(out=out, in0=a, scalar=scalar, in1=b, op0=op0, op1=op1), R, W)

        def cp(eng, out, in_, R, W):
            if eng == "act":
                return S.op("act", lambda e: e.activation(out=out, in_=in_, func=AF.Copy), R, W)
            return S.op(eng, lambda e: e.tensor_copy(out=out, in_=in_), R, W)

        def dma(q, out, in_, R, W):
            return S.dma(q, lambda e: e.dma_start(out=out, in_=in_), R, W)

        def memset(eng, ap, val, W):
            return S.op(eng, lambda e: e.memset(ap, val), (), W)

        def bc(ap, shape):
            return ap.broadcast_to(list(shape))

        rr = {"ps": 0, "ev": 0}

        ident_f, ident_f_b = SM.alloc("ident_f", (P,), F32)
        ident_b, ident_b_b = SM.alloc("ident_b", (P,), BF16)
        ones_b, ones_b_b = SM.alloc("ones_b", (P,), BF16)
        ones_f, ones_f_b = SM.alloc("ones_f", (P,), F32)
        sc, sc_b = SM.alloc("sc", (KC, 2), F32)
        nhalf, nhalf_b = SM.alloc("nhalf", (8,), F32)
        modT, modT_b = SM.alloc("modT", (48, 2), F32)
        gcol, gcol_b = SM.alloc("gcol", (KC, 2), F32)
        nwT, nwT_b = SM.alloc("nwT", (KC,), F32)
        badaT, badaT_b = SM.alloc("badaT", (48,), F32)
        bgT, bgT_b = SM.alloc("bgT", (48,), F32)
        lamc, lamc_b = SM.alloc("lamc", (8,), F32)
        lamin, lamin_b = SM.alloc("lamin", (4, 64), F32)
        sublc, sublc_b = SM.alloc("sublc", (2,), F32)
        stat = [SM.alloc("stat%d" % i, (16,), F32) for i in range(6)]

        dma("sp", ident_f, ident_d, [], [ident_f_b])
        dma("pool", ident_b, ident_d, [], [ident_b_b])
        memset("dve", ones_b, 1.0, [ones_b_b])
        memset("dve", ones_f, 1.0, [ones_f_b])
        memset("dve", nhalf, -0.5, [nhalf_b])
        dma("sp", sc, ccT, [], [sc_b])
        act(sc, sc, AF.Silu, [sc_b], [sc_b])

        Xcur = xin
        Xcur_b = [Buf("xin_%d" % i) for i in range(NT)]

        def ckpt(name):
            if stop == name:
                S.kill()

        for li in range(nl):
            labs = first_layer + li
            last_mod_layer = (labs == DEPTH - 1)
            upd_ctx = not last_mod_layer
            lam_init = 0.8 - 0.6 * math.exp(-0.3 * labs)

            mk = A.mark()
            dma("sp", nwT, norm_wT[li], [], [nwT_b])
            dma("sp", badaT, b_adaT[li], [], [badaT_b])
            dma("sp", bgT, b_gateT[li], [], [bgT_b])
            dma("sp", lamin, lamv[li], [], [lamin_b])
            dma("sp", sublc[:, 0:1], sublnT[li], [], [sublc_b])
            wa = [A.alloc("wa%d" % i, (KC, 512), F32) for i in range(2)]
            wav = w_ada[li].rearrange("(kc p) n -> p kc n", p=P)
            pA, pAb = ps[0], pb[0]
            for c in range(12):
                wt_, wb_ = wa[c % 2]
                dma("sp", wt_, wav[:, :, c * 512:(c + 1) * 512], [], [wb_])
                for sub in range(4):
                    n = c * 4 + sub
                    for kc in range(KC):
                        mm(pA[:, n * 2:(n + 1) * 2], wt_[:, kc, sub * P:(sub + 1) * P], sc[:, kc, :],
                           kc == 0, kc == KC - 1, [wb_, sc_b], [pAb])
            tt("dve", modT, pA[:, 0:96].rearrange("p (a b) -> p a b", b=2),
               bc(badaT.rearrange("p (a b) -> p a b", b=1), (P, 48, 2)), ALU.add, [pAb, badaT_b], [modT_b])
            stt(gcol, modT[:, 16:32, :], 1.0, bc(nwT.rearrange("p (a b) -> p a b", b=1), (P, KC, 2)),
                ALU.add, ALU.mult, [modT_b, nwT_b], [gcol_b])
            st0, st0b = stat[0]
            tt("dve", lamin[:, 0, :], lamin[:, 0, :], lamin[:, 1, :], ALU.mult, [lamin_b], [lamin_b])
            tt("dve", lamin[:, 2, :], lamin[:, 2, :], lamin[:, 3, :], ALU.mult, [lamin_b], [lamin_b])
            S.op("dve", lambda e, o=st0[:, 0:1], i=lamin[:, 0, :]: e.tensor_reduce(out=o, in_=i, axis=AX.X, op=ALU.add), [lamin_b], [st0b])
            S.op("dve", lambda e, o=st0[:, 1:2], i=lamin[:, 2, :]: e.tensor_reduce(out=o, in_=i, axis=AX.X, op=ALU.add), [lamin_b], [st0b])
            act(st0[:, 2:4], st0[:, 0:2], AF.Exp, [st0b], [st0b])
            stt(lamc[:, 0:1], st0[:, 3:4], -lam_init, st0[:, 2:3], ALU.add, ALU.subtract, [st0b], [lamc_b])
            ts("dve", lamc[:, 1:2], sublc[:, 0:1], 1.0 - lam_init, ALU.mult, [sublc_b], [lamc_b])
            A.release(mk)
            ckpt("ada")

            mk = A.mark()
            HTN = 1792
            hT, _ = A.alloc("hT", (KC, HTN), BF16)
            hT_b = [A.extra_buf("hT_%d" % i) for i in range(HTN // P)]
            NWT = 3
            wts = [A.alloc("wt%d" % i, (KC, 512), BF16) for i in range(NWT)]
            xts = [A.alloc("xt%d" % i, (D,), F32) for i in range(2)]
            xn, xn_b = A.alloc("xn", (D,), F32)
            junk, junk_b = A.alloc("junk", (D,), BF16)
            stF = [A.alloc("stF%d" % i, (4, 512), BF16) for i in range(2)]
            stV = [A.alloc("stV%d" % i, (512,), BF16) for i in range(2)]
            stT = [A.alloc("stT%d" % i, (4, 512), BF16) for i in range(2)]
            sq, sq_b = A.alloc("sq", (512,), F32)
            yb = [A.alloc("yb%d" % i, (512,), F32) for i in range(2)]
            tmp1 = [A.alloc("tmpA%d" % i, (256,), F32) for i in range(2)]
            tmp2 = [A.alloc("tmpB%d" % i, (256,), F32) for i in range(2)]
            tmp3 = [A.alloc("tmpC%d" % i, (256,), F32) for i in range(2)]
            tmp4 = [A.alloc("tmpD%d" % i, (256,), F32) for i in range(2)]
            ob = [A.alloc("ob%d" % i, (512,), BF16) for i in range(2)]
            cosb, cosb_b = A.alloc("cosb", (32, 64), F32)
            sinb, sinb_b = A.alloc("sinb", (32, 64), F32)
            cosc, cosc_b = A.alloc("cosc", (32, 32), F32)
            sinc, sinc_b = A.alloc("sinc", (32, 32), F32)
            qnw_t, qnw_b = A.alloc("qnw", (HD,), F32)
            knw_t, knw_b = A.alloc("knw", (HD,), F32)
            dma("sp", cosb, cosb_d, [], [cosb_b])
            dma("sp", sinb, sinb_d, [], [sinb_b])
            dma("sp", cosc, cosc_d, [], [cosc_b])
            dma("sp", sinc, sinc_d, [], [sinc_b])
            dma("sp", qnw_t, qnw[li], [], [qnw_b])
            dma("sp", knw_t, knw[li], [], [knw_b])
            winv = w_in[li].rearrange("(kc p) n -> p kc n", p=P)

            CH = {}
            CH[0] = ("Fcopy", QAT, 0)
            CH[1] = ("Fcopy", KAT, 0)
            CH[2] = ("Tv", [(VA, 0, 0, 512)], None)
            CH[3] = ("Fsilu", SZT, 0)
            CH[4] = ("Tqk", [("b", QBT, 0, 0, 4, qnw_t, qnw_b)], None)
            CH[5] = ("Tqk", [("b", QBT, 512, 0, 4, qnw_t, qnw_b)], None)
            CH[6] = ("Tmix", None, None)
            CH[7] = ("Fsilu", SZT, 512)
            CH[8] = ("Fsilu", SZT, 1024)
            CH[9] = ("Tqk", [("c", QCT, 0, 0, 4, None, None)], None)
            CH[10] = ("Tqk", [("c", KCT, 0, 0, 4, None, None)], None)
            CH[11] = ("Tv", [(VC, 0, 0, 512)], None)
            CH[12] = ("Fsilu", SZT, 1536)
            for i in range(12):
                CH[13 + i] = ("Fgate", GT, 512 * i)

            groups = [MT[0:4], MT[4:7], MT[7:9]]
            xcount = 0
            pcount = 0
            w_items = [(gi_, c_) for gi_ in range(len(groups)) for c_ in range(25)]
            w_state = {"n": 0}

            def ensure_w(upto):
                while w_state["n"] <= min(upto, len(w_items) - 1):
                    i_ = w_state["n"]
                    w_t, w_b = wts[i_ % NWT]
                    c_ = w_items[i_][1]
                    dma("pool", w_t, winv[:, :, c_ * 512:(c_ + 1) * 512], [], [w_b])
                    w_state["n"] += 1

            for gi, grp in enumerate(groups):
                ensure_w(gi * 25 + 1)
                g_tok0 = grp[0][0]
                g_n = sum(n for _, n in grp)
                for tl in range(g_n // P):
                    tg = (g_tok0 // P) + tl
                    r = 1 if tg < 2 else 0
                    xt_, xtb_ = xts[xcount % 2]
                    sst, sstb = stat[1 + xcount % 2]
                    xcount += 1
                    dma("sp", xt_, Xcur[tg * P:(tg + 1) * P, :], [Xcur_b[tg]], [xtb_])
                    act(junk, xt_, AF.Square, [xtb_], [junk_b, sstb], accum=sst[:, 0:1])
                    ts("dve", sst[:, 1:2], sst[:, 0:1], 1.0 / D, ALU.mult, [sstb], [sstb], s2=EPS, op1=ALU.add)
                    tt("pool", sst[:, 2:3], sst[:, 1:2], nhalf[:, 0:1], ALU.pow, [sstb, nhalf_b], [sstb])
                    ts("dve", xn, xt_, sst[:, 2:3], ALU.mult, [xtb_, sstb], [xn_b])
                    for q4 in range(4):
                        bk = 6 + (pcount % 2)
                        pcount += 1
                        for j in range(4):
                            kc = q4 * 4 + j
                            tr(ps[bk][:, j * P:(j + 1) * P], xn[:, kc * P:(kc + 1) * P], ident_f, [xn_b, ident_f_b], [pb[bk]])
                        for j in range(4):
                            kc = q4 * 4 + j
                            o_ = hT[:, kc, tl * P:(tl + 1) * P]
                            i_ = ps[bk][:, j * P:(j + 1) * P]
                            if bk == 6:
                                act(o_, i_, AF.Identity, [pb[bk], gcol_b, modT_b], [hT_b[tl]],
                                    bias=modT[:, kc, r:r + 1], scale=gcol[:, kc, r:r + 1])
                            else:
                                ts("dve", o_, i_, gcol[:, kc, r:r + 1], ALU.mult, [pb[bk], gcol_b, modT_b], [hT_b[tl]],
                                   s2=modT[:, kc, r:r + 1], op1=ALU.add)
                ckpt("ab_a%d" % gi)
                mts = []
                loc = 0
                for (t0, n) in grp:
                    mts.append((t0, n, loc))
                    loc += n
                for c in range(25):
                    ckpt("ab_g%dc%d" % (gi, c))
                    widx = gi * 25 + c
                    ensure_w(widx + 2)
                    wt_, wb_ = wts[widx % NWT]
                    kind = CH[c][0]
                    if kind[0] == "F":
                        dst, row0 = CH[c][1], CH[c][2]
                        for (t0, n, loc) in mts:
                            stg, stgb = stF[rr["ev"] % 2]
                            rr["ev"] += 1
                            hb = [hT_b[(loc + i) // P] for i in range(0, n, P)]
                            for sub in range(4):
                                bk = rr["ps"] % 6
                                rr["ps"] += 1
                                for kc in range(KC):
                                    mm(ps[bk][:, 0:n], wt_[:, kc, sub * P:(sub + 1) * P], hT[:, kc, loc:loc + n],
                                       kc == 0, kc == KC - 1, [wb_] + hb, [pb[bk]])
                                if kind == "Fcopy":
                                    cp("dve", stg[:, sub, 0:n], ps[bk][:, 0:n], [pb[bk]], [stgb])
                                elif kind == "Fsilu":
                                    act(stg[:, sub, 0:n], ps[bk][:, 0:n], AF.Silu, [pb[bk]], [stgb])
                                else:
                                    gidx = (row0 // P) + sub
                                    act(stg[:, sub, 0:n], ps[bk][:, 0:n], AF.Sigmoid, [pb[bk], bgT_b], [stgb],
                                        bias=bgT[:, gidx:gidx + 1])
                            dma("pool", dst.ap[row0:row0 + 512, t0:t0 + n].rearrange("(s p) t -> p s t", p=P),
                                stg[:, :, 0:n], [stgb], dst.bufs(t0, n))
                    else:
                        if kind == "Tv":
                            parts = [("v",) + CH[c][1][0]]
                        elif kind == "Tqk":
                            parts = CH[c][1]
                        else:
                            parts = [("b", KBT, 0, 0, 2, knw_t, knw_b), ("v", VB, 0, 256, 256)]
                        for (t0, n, loc) in mts:
                            stt_, sttb = stT[rr["ev"] % 2]
                            rr["ev"] += 1
                            for tsub in range(n // P):
                                tg = (t0 // P) + tsub
                                is_lat = tg >= 2
                                lt = tg - 2
                                bk = rr["ps"] % 6
                                rr["ps"] += 1
                                tl = (loc // P) + tsub
                                for kc in range(KC):
                                    mm(ps[bk][:, :], hT[:, kc, tl * P:(tl + 1) * P], wt_[:, kc, :],
                                       kc == 0, kc == KC - 1, [wb_, hT_b[tl]], [pb[bk]])
                                for part in parts:
                                    if part[0] == "v":
                                        _, dstv, dcol0, pcol0, ncol = part
                                        sv, svb = stV[rr["ev"] % 2]
                                        rr["ev"] += 1
                                        cp("act", sv[:, 0:ncol], ps[bk][:, pcol0:pcol0 + ncol], [pb[bk]], [svb])
                                        dma("pool", dstv.ap[tg * P:(tg + 1) * P, dcol0:dcol0 + ncol], sv[:, 0:ncol],
                                            [svb], dstv.bufs(tg * P, P))
                                        continue
                                    typ, dstq, drow0, pcol0, nh, nw_t, nw_b = part
                                    ncol = nh * HD
                                    src = ps[bk][:, pcol0:pcol0 + ncol]
                                    o_, o_b = ob[rr["ev"] % 2]
                                    y_, y_b = yb[rr["ev"] % 2]
                                    t1, t1b = tmp1[rr["ev"] % 2]
                                    t2, t2b = tmp2[rr["ev"] % 2]
                                    t3, t3b = tmp3[rr["ev"] % 2]
                                    t4, t4b = tmp4[rr["ev"] % 2]
                                    rr["ev"] += 1
                                    if typ == "b":
                                        sst, sstb = stat[3 + rr["ev"] % 2]
                                        act(sq[:, 0:ncol], src, AF.Square, [pb[bk]], [sq_b])
                                        S.op("dve", lambda e, o=sst[:, 0:nh], i=sq[:, 0:ncol].rearrange("p (h d) -> p h d", d=HD):
                                             e.tensor_reduce(out=o, in_=i, axis=AX.X, op=ALU.add), [sq_b], [sstb])
                                        ts("dve", sst[:, 4:4 + nh], sst[:, 0:nh], 1.0 / HD, ALU.mult, [sstb], [sstb], s2=EPS, op1=ALU.add)
                                        tt("pool", sst[:, 8:8 + nh], sst[:, 4:4 + nh], nhalf[:, 0:nh], ALU.pow, [sstb, nhalf_b], [sstb])
                                        y3 = y_[:, 0:ncol].rearrange("p (h d) -> p h d", d=HD)
                                        tt("dve", y3, src.rearrange("p (h d) -> p h d", d=HD),
                                           bc(sst[:, 8:8 + nh].rearrange("p (h o) -> p h o", o=1), (P, nh, HD)), ALU.mult,
                                           [pb[bk], sstb], [y_b])
                                        tt("pool", y3, y3, bc(nw_t.rearrange("p (o d) -> p o d", o=1), (P, nh, HD)), ALU.mult,
                                           [y_b, nw_b], [y_b])
                                        ysrc, ysrc_b = y_[:, 0:ncol], y_b
                                        ngrp, half = nh, 64
                                        cs_t, cs_b, sn_t, sn_b = cosb, cosb_b, sinb, sinb_b
                                    else:
                                        ysrc, ysrc_b = src, pb[bk]
                                        ngrp, half = nh * 2, 32
                                        cs_t, cs_b, sn_t, sn_b = cosc, cosc_b, sinc, sinc_b
                                    if is_lat:
                                        yv = ysrc.rearrange("p (g t d) -> p g t d", t=2, d=half)
                                        ov = o_[:, 0:ncol].rearrange("p (g t d) -> p g t d", t=2, d=half)
                                        y1, y2 = yv[:, :, 0, :], yv[:, :, 1, :]
                                        cs = bc(cs_t[:, lt, :].rearrange("p (o d) -> p o d", o=1), (P, ngrp, half))
                                        sn = bc(sn_t[:, lt, :].rearrange("p (o d) -> p o d", o=1), (P, ngrp, half))
                                        nel = ngrp * half
                                        v3 = lambda t_: t_[:, 0:nel].rearrange("p (g d) -> p g d", d=half)
                                        tt("dve", v3(t1), y1, cs, ALU.mult, [ysrc_b, cs_b], [t1b])
                                        tt("dve", v3(t2), y2, sn, ALU.mult, [ysrc_b, sn_b], [t2b])
                                        tt("dve", v3(t3), y1, sn, ALU.mult, [ysrc_b, sn_b], [t3b])
                                        tt("dve", v3(t4), y2, cs, ALU.mult, [ysrc_b, cs_b], [t4b])
                                        tt("pool", ov[:, :, 0, :], v3(t1), v3(t2), ALU.subtract, [t1b, t2b], [o_b])
                                        tt("pool", ov[:, :, 1, :], v3(t3), v3(t4), ALU.add, [t3b, t4b], [o_b])
                                    else:
                                        cp("dve", o_[:, 0:ncol], ysrc, [ysrc_b], [o_b])
                                    bkT = 6 + (pcount % 2)
                                    pcount += 1
                                    pT = ps[bkT][:].bitcast(BF16)
                                    for h in range(nh):
                                        tr(pT[:, h * P:(h + 1) * P], o_[:, h * P:(h + 1) * P], ident_b, [o_b, ident_b_b], [pb[bkT]])
                                    hoff = (drow0 % 512) // P
                                    cp("dve", stt_[:, hoff:hoff + nh, tsub * P:(tsub + 1) * P],
                                       pT[:, 0:nh * P].rearrange("p (h t) -> p h t", t=P), [pb[bkT]], [sttb])
                            for part in parts:
                                if part[0] == "v":
                                    continue
                                typ, dstq, drow0, pcol0, nh, nw_t, nw_b = part
                                hoff = (drow0 % 512) // P
                                dma("pool", dstq.ap[drow0:drow0 + nh * P, t0:t0 + n].rearrange("(s p) t -> p s t", p=P),
                                    stt_[:, hoff:hoff + nh, 0:n], [sttb], dstq.bufs(t0, n))
            A.release(mk)
            ckpt("ab")

            q_blocks = [(256 + 512 * i, 512, True) for i in range(8)]
            if upd_ctx:
                q_blocks = q_blocks + [(0, 256, False)]

            def finish_block(o_bank, s_bank, n, szt, szb, ust, ustb, extra=None):
                rec, recb = extra
                S.op("dve", lambda e: e.reciprocal(out=rec[:, 0:n], in_=ps[s_bank][:, 0:n]), [pb[s_bank]], [recb])
                tt("dve", rec[:, 0:n], ps[o_bank][:, 0:n], rec[:, 0:n], ALU.mult, [pb[o_bank], recb], [recb])
                tt("pool", ust[:, 0:n], rec[:, 0:n], szt[:, 0:n], ALU.mult, [recb, szb], [ustb])

            mk = A.mark()
            sclA = HD ** -0.5
            KT_, KT_b = A.alloc("KTa", (4, T), BF16)
            V_, V_b = A.alloc("Va", (NT, 512), BF16)
            bA, bA_b = A.alloc("biasA", (4, NB_A, P), F32)
            Qs = [A.alloc("Qa%d" % i, (512,), BF16) for i in range(2)]
            SZs = [A.alloc("SZa%d" % i, (512,), BF16) for i in range(2)]
            Pt = [A.alloc("Pa%d" % i, (7, P), BF16) for i in range(3)]
            tmpS = [A.alloc("tSa%d" % i, (5, P), F32) for i in range(2)]
            recs = [A.alloc("reca%d" % i, (512,), F32) for i in range(2)]
            usts = [A.alloc("usta%d" % i, (512,), BF16) for i in range(2)]
            dma("sp", KT_, KAT.ap.rearrange("(h p) t -> p h t", p=P), KAT.b, [KT_b])
            dma("sp", V_, VA.ap.rearrange("(t p) c -> p t c", p=P), VA.b, [V_b])
            dma("sp", bA, biasA[li], [], [bA_b])
            blk = 0
            for h in range(4):
                for (q0, qn, is_lat) in q_blocks:
                    Qt, Qb = Qs[blk % 2]
                    SZ, SZb = SZs[blk % 2]
                    rc = recs[blk % 2]
                    ust, ustb = usts[blk % 2]
                    blk += 1
                    dma("sp", Qt[:, 0:qn], QAT.ap[h * P:(h + 1) * P, q0:q0 + qn], QAT.bufs(q0, qn), [Qb])
                    dma("sp", SZ[:, 0:qn], SZT.ap[h * P:(h + 1) * P, q0:q0 + qn], SZT.bufs(q0, qn), [SZb])
                    o_bank, s_bank = 4, 5
                    if is_lat:
                        for i in range(4):
                            j = (q0 - LC) // P + i
                            ent = _NPLAN[j]
                            nw = len(ent)
                            Pq, Pqb = Pt[rr["ev"] % 3]
                            tS, tSb = tmpS[rr["ev"] % 2]
                            rr["ev"] += 1
                            qcol = Qt[:, i * P:(i + 1) * P]
                            b0 = (rr["ps"] % 2) * 2
                            rr["ps"] += 1
                            for wi, (jj, bi) in enumerate(ent):
                                kt = 2 + jj
                                if wi < 4:
                                    o_ps = ps[b0][:, wi * P:(wi + 1) * P]
                                    bkk = b0
                                else:
                                    o_ps = ps[b0 + 1][:, 0:P]
                                    bkk = b0 + 1
                                mm(o_ps, KT_[:, h, kt * P:(kt + 1) * P], qcol, True, True, [KT_b, Qb], [pb[bkk]])
                            for ci in range(2):
                                mm(ps[b0 + 1][:, (1 + ci) * P:(2 + ci) * P], KT_[:, h, ci * P:(ci + 1) * P], qcol, True, True,
                                   [KT_b, Qb], [pb[b0 + 1]])
                            bi0 = ent[0][1]
                            assert [e[1] for e in ent] == list(range(bi0, bi0 + nw))
                            n4 = min(nw, 4)
                            stt(tS[:, 0:n4, :], ps[b0][:, 0:n4 * P].rearrange("p (a b) -> p a b", b=P), sclA,
                                bA[:, h, bi0:bi0 + n4, :], ALU.mult, ALU.add, [pb[b0], bA_b], [tSb])
                            if nw == 5:
                                stt(tS[:, 4, :], ps[b0 + 1][:, 0:P], sclA, bA[:, h, bi0 + 4, :], ALU.mult, ALU.add,
                                    [pb[b0 + 1], bA_b], [tSb])
                            act(Pq[:, 0:nw, :], tS[:, 0:nw, :], AF.Exp, [tSb], [Pqb])
                            act(Pq[:, 5:7, :], ps[b0 + 1][:, P:3 * P].rearrange("p (a b) -> p a b", b=P), AF.Exp,
                                [pb[b0 + 1]], [Pqb], scale=sclA)
                            tiles = [(2 + jj, wi) for wi, (jj, bi) in enumerate(ent)] + [(0, 5), (1, 6)]
                            for ti, (kt, pi) in enumerate(tiles):
                                mm(ps[o_bank][:, i * P:(i + 1) * P], V_[:, kt, h * P:(h + 1) * P], Pq[:, pi, :],
                                   ti == 0, ti == len(tiles) - 1, [V_b, Pqb], [pb[o_bank]])
                            for ti, (kt, pi) in enumerate(tiles):
                                mm(ps[s_bank][:, i * P:(i + 1) * P], ones_b, Pq[:, pi, :],
                                   ti == 0, ti == len(tiles) - 1, [ones_b_b, Pqb], [pb[s_bank]])
                    else:
                        Pq, Pqb = Pt[rr["ev"] % 3]
                        rr["ev"] += 1
                        b0 = (rr["ps"] % 2) * 2
                        rr["ps"] += 1
                        for ci in range(2):
                            mm(ps[b0][:, ci * 256:(ci + 1) * 256], KT_[:, h, ci * P:(ci + 1) * P], Qt[:, 0:256], True, True,
                               [KT_b, Qb], [pb[b0]])
                        Pv = Pq[:, 0:4, :].rearrange("p a b -> p (a b)")
                        act(Pv, ps[b0][:, 0:512], AF.Exp, [pb[b0]], [Pqb], scale=sclA)
                        for ci in range(2):
                            mm(ps[o_bank][:, 0:256], V_[:, ci, h * P:(h + 1) * P], Pv[:, ci * 256:(ci + 1) * 256],
                               ci == 0, ci == 1, [V_b, Pqb], [pb[o_bank]])
                        for ci in range(2):
                            mm(ps[s_bank][:, 0:256], ones_b, Pv[:, ci * 256:(ci + 1) * 256],
                               ci == 0, ci == 1, [ones_b_b, Pqb], [pb[s_bank]])
                    finish_block(o_bank, s_bank, qn, SZ, SZb, ust, ustb, extra=rc)
                    dma("pool", UT.ap[h * P:(h + 1) * P, q0:q0 + qn], ust[:, 0:qn], [ustb], UT.bufs(q0, qn))
            A.release(mk)
            ckpt("attA")

            mk = A.mark()
            sclB = HD ** -0.5
            KT_, KT_b = A.alloc("KTb", (2, T), BF16)
            V_, V_b = A.alloc("Vb", (NT, 256), BF16)
            Qs = [A.alloc("Qb%d" % i, (4, 512), BF16) for i in range(2)]
            SZs = [A.alloc("SZb%d" % i, (4, 512), BF16) for i in range(2)]
            Pt = [A.alloc("Pb%d" % i, (512,), BF16) for i in range(3)]
            recs = [A.alloc("recb%d" % i, (512,), F32) for i in range(2)]
            usts = [A.alloc("ustb%d" % i, (4, 512), BF16) for i in range(2)]
            dma("sp", KT_, KBT.ap.rearrange("(h p) t -> p h t", p=P), KBT.b, [KT_b])
            dma("sp", V_, VB.ap.rearrange("(t p) c -> p t c", p=P), VB.b, [V_b])
            blk = 0
            for g in range(2):
                for (q0, qn, is_lat) in q_blocks:
                    Qt, Qb = Qs[blk % 2]
                    SZ, SZb = SZs[blk % 2]
                    ust, ustb = usts[blk % 2]
                    blk += 1
                    r0 = g * 512
                    dma("sp", Qt[:, :, 0:qn], QBT.ap[r0:r0 + 512, q0:q0 + qn].rearrange("(r p) t -> p r t", p=P),
                        QBT.bufs(q0, qn), [Qb])
                    dma("sp", SZ[:, :, 0:qn], SZT.ap[512 + r0:512 + r0 + 512, q0:q0 + qn].rearrange("(r p) t -> p r t", p=P),
                        SZT.bufs(q0, qn), [SZb])
                    kts = list(range(NT)) if is_lat else [0, 1]
                    for qs in range(qn // P):
                        rhs = Qt[:, :, qs * P:(qs + 1) * P]
                        o_bank, s_bank = 4 + 2 * (qs % 2), 5 + 2 * (qs % 2)
                        rc = recs[qs % 2]
                        pend = None
                        for ki, kt in enumerate(kts):
                            bk = rr["ps"] % 4
                            rr["ps"] += 1
                            mm(ps[bk][:, :], KT_[:, g, kt * P:(kt + 1) * P], rhs, True, True, [KT_b, Qb], [pb[bk]])
                            if pend is not None:
                                pki, pkt, pP, pPb = pend
                                mm(ps[o_bank][:, :], V_[:, pkt, g * P:(g + 1) * P], pP, pki == 0, False, [V_b, pPb], [pb[o_bank]])
                                mm(ps[s_bank][:, :], ones_b, pP, pki == 0, False, [ones_b_b, pPb], [pb[s_bank]])
                            Pq, Pqb = Pt[rr["ev"] % 3]
                            rr["ev"] += 1
                            act(Pq, ps[bk][:, :], AF.Exp, [pb[bk]], [Pqb], scale=sclB)
                            pend = (ki, kt, Pq, Pqb)
                        pki, pkt, pP, pPb = pend
                        mm(ps[o_bank][:, :], V_[:, pkt, g * P:(g + 1) * P], pP, pki == 0, True, [V_b, pPb], [pb[o_bank]])
                        mm(ps[s_bank][:, :], ones_b, pP, pki == 0, True, [ones_b_b, pPb], [pb[s_bank]])
                        rec, recb = rc
                        S.op("dve", lambda e, o=rec, i=ps[s_bank][:, :]: e.reciprocal(out=o, in_=i), [pb[s_bank]], [recb])
                        tt("dve", rec, ps[o_bank][:, :], rec, ALU.mult, [pb[o_bank], recb], [recb])
                        tt("pool", ust[:, :, qs * P:(qs + 1) * P], rec.rearrange("p (r q) -> p r q", q=P),
                           SZ[:, :, qs * P:(qs + 1) * P], ALU.mult, [recb, SZb], [ustb])
                    dma("pool", UT.ap[512 + r0:512 + r0 + 512, q0:q0 + qn].rearrange("(r p) t -> p r t", p=P),
                        ust[:, :, 0:qn], [ustb], UT.bufs(q0, qn))
            A.release(mk)
            ckpt("attB")

            mk = A.mark()
            sclC = 64 ** -0.5
            KT_, KT_b = A.alloc("KTc", (4, T), BF16)
            V_, V_b = A.alloc("Vc", (NT, 512), BF16)
            Qs = [A.alloc("Qc%d" % i, (512,), BF16) for i in range(2)]
            SZs = [A.alloc("SZc%d" % i, (512,), BF16) for i in range(2)]
            Pt = [A.alloc("Pc%d" % i, (2, 512), BF16) for i in range(3)]
            r0t, r0b = A.alloc("rc0", (512,), F32)
            r1t, r1b = A.alloc("rc1", (512,), F32)
            o2t, o2b = A.alloc("oc2", (512,), F32)
            usts = [A.alloc("ustc%d" % i, (512,), BF16) for i in range(2)]
            dma("sp", KT_, KCT.ap.rearrange("(h p) t -> p h t", p=P), KCT.b, [KT_b])
            dma("sp", V_, VC.ap.rearrange("(t p) c -> p t c", p=P), VC.b, [V_b])
            blk = 0
            for h in range(4):
                for (q0, qn, is_lat) in q_blocks:
                    Qt, Qb = Qs[blk % 2]
                    SZ, SZb = SZs[blk % 2]
                    ust, ustb = usts[blk % 2]
                    blk += 1
                    dma("sp", Qt[:, 0:qn], QCT.ap[h * P:(h + 1) * P, q0:q0 + qn], QCT.bufs(q0, qn), [Qb])
                    dma("sp", SZ[:, 0:qn], SZT.ap[1536 + h * P:1536 + (h + 1) * P, q0:q0 + qn], SZT.bufs(q0, qn), [SZb])
                    kts = list(range(NT)) if is_lat else [0, 1]
                    O0, O1, S0, S1 = 4, 5, 6, 7
                    pend = None
                    for ki, kt in enumerate(kts):
                        b0 = (rr["ps"] % 2) * 2
                        rr["ps"] += 1
                        for m in range(2):
                            mm(ps[b0 + m][:, 0:qn], KT_[m * 64:(m + 1) * 64, h, kt * P:(kt + 1) * P], Qt[m * 64:(m + 1) * 64, 0:qn],
                               True, True, [KT_b, Qb], [pb[b0 + m]])
                        if pend is not None:
                            pki, pkt, pP, pPb = pend
                            for m, (ob_, sb_) in enumerate(((O0, S0), (O1, S1))):
                                mm(ps[ob_][:, 0:qn], V_[:, pkt, h * P:(h + 1) * P], pP[:, m, 0:qn], pki == 0, False, [V_b, pPb], [pb[ob_]])
                                mm(ps[sb_][:, 0:qn], ones_b, pP[:, m, 0:qn], pki == 0, False, [ones_b_b, pPb], [pb[sb_]])
                        Pq, Pqb = Pt[rr["ev"] % 3]
                        rr["ev"] += 1
                        for m in range(2):
                            act(Pq[:, m, 0:qn], ps[b0 + m][:, 0:qn], AF.Exp, [pb[b0 + m]], [Pqb], scale=sclC)
                        pend = (ki, kt, Pq, Pqb)
                    pki, pkt, pP, pPb = pend
                    for m, (ob_, sb_) in enumerate(((O0, S0), (O1, S1))):
                        mm(ps[ob_][:, 0:qn], V_[:, pkt, h * P:(h + 1) * P], pP[:, m, 0:qn], pki == 0, True, [V_b, pPb], [pb[ob_]])
                        mm(ps[sb_][:, 0:qn], ones_b, pP[:, m, 0:qn], pki == 0, True, [ones_b_b, pPb], [pb[sb_]])
                    S.op("dve", lambda e, o=r0t[:, 0:qn], i=ps[S0][:, 0:qn]: e.reciprocal(out=o, in_=i), [pb[S0]], [r0b])
                    S.op("dve", lambda e, o=r1t[:, 0:qn], i=ps[S1][:, 0:qn]: e.reciprocal(out=o, in_=i), [pb[S1]], [r1b])
                    tt("dve", r0t[:, 0:qn], ps[O0][:, 0:qn], r0t[:, 0:qn], ALU.mult, [pb[O0], r0b], [r0b])
                    tt("dve", r1t[:, 0:qn], ps[O1][:, 0:qn], r1t[:, 0:qn], ALU.mult, [pb[O1], r1b], [r1b])
                    stt(r0t[:, 0:qn], r1t[:, 0:qn], lamc[:, 0:1], r0t[:, 0:qn], ALU.mult, ALU.add, [r0b, r1b, lamc_b], [r0b])
                    tt("pool", o2t[:, 0:qn], r0t[:, 0:qn], r0t[:, 0:qn], ALU.mult, [r0b], [o2b])
                    bq_ = (rr["ps"] % 2) * 2
                    rr["ps"] += 1
                    mm(ps[bq_][:, 0:qn], ones_f, o2t[:, 0:qn], True, True, [ones_f_b, o2b], [pb[bq_]])
                    ts("dve", r1t[:, 0:qn], ps[bq_][:, 0:qn], 1.0 / HD, ALU.mult, [pb[bq_]], [r1b], s2=EPS, op1=ALU.add)
                    tt("pool", r1t[:, 0:qn], r1t[:, 0:qn], bc(nhalf[:, 0:1], (P, qn)), ALU.pow, [r1b, nhalf_b], [r1b])
                    stt(r0t[:, 0:qn], r0t[:, 0:qn], lamc[:, 1:2], r1t[:, 0:qn], ALU.mult, ALU.mult, [r0b, r1b, lamc_b], [r0b])
                    tt("pool", ust[:, 0:qn], r0t[:, 0:qn], SZ[:, 0:qn], ALU.mult, [r0b, SZb], [ustb])
                    dma("pool", UT.ap[1536 + h * P:1536 + (h + 1) * P, q0:q0 + qn], ust[:, 0:qn], [ustb], UT.bufs(q0, qn))
            A.release(mk)
            ckpt("attC")

            mk = A.mark()
            Ug, Ug_b = A.alloc("Ug", (KC, 1024), BF16)
            yT, _ = A.alloc("yT", (KC, 1024), BF16)
            yT_b = [A.extra_buf("yT_%d" % i) for i in range(8)]
            wbs = [A.alloc("wbo%d" % i, (KC, 512), BF16) for i in range(2)]
            wos = [A.alloc("wo%d" % i, (KC, 512), BF16) for i in range(2)]
            gbc = [A.alloc("gbc%d" % i, (D,), F32) for i in range(2)]
            gts = [A.alloc("gt%d" % i, (3, 512), BF16) for i in range(3)]
            m1s = [A.alloc("m1_%d" % i, (512,), F32) for i in range(2)]
            m2s = [A.alloc("m2_%d" % i, (512,), F32) for i in range(2)]
            m3s = [A.alloc("m3_%d" % i, (512,), F32) for i in range(2)]
            xrs = [A.alloc("xr%d" % i, (512,), F32) for i in range(3)]
            xos = [A.alloc("xo%d" % i, (512,), F32) for i in range(3)]
            dg, dg_b = A.alloc("dg", (P,), F32)
            for r in range(2):
                if r == 1 and not upd_ctx:
                    continue
                for kc in range(KC):
                    bk = rr["ps"] % 6
                    rr["ps"] += 1
                    ts("dve", dg, ident_f, modT[:, 32 + kc, r:r + 1], ALU.mult, [ident_f_b, modT_b], [dg_b])
                    mm(ps[bk][:, 0:P], ones_f, dg, True, True, [ones_f_b, dg_b], [pb[bk]])
                    cp("act", gbc[r][0][:, kc * P:(kc + 1) * P], ps[bk][:, 0:P], [pb[bk]], [gbc[r][1]])
            wbov = w_bo[li].rearrange("(kc p) n -> p kc n", p=P)
            wov = w_out[li].rearrange("(kc p) n -> p kc n", p=P)
            dgroups = [(256 + 1024 * i, 1024, 0) for i in range(4)]
            if upd_ctx:
                dgroups.append((0, 256, 1))
            Xnext = (xout if not is_final else Xs) if li == nl - 1 else Xs
            Xnext_b = (xo_b if not is_final else xs_b) if li == nl - 1 else xs_b
            wall = wbs + wos
            d_items = [(gi_, kind_, j_) for gi_ in range(len(dgroups)) for kind_ in ("bo", "wo") for j_ in range(4)]
            d_state = {"n": 0}

            def ensure_d(upto):
                while d_state["n"] <= min(upto, len(d_items) - 1):
                    i_ = d_state["n"]
                    w_t, w_b = wall[i_ % 4]
                    _, kind_, j_ = d_items[i_]
                    srcv = wbov if kind_ == "bo" else wov
                    dma("pool", w_t, srcv[:, :, j_ * 512:(j_ + 1) * 512], [], [w_b])
                    d_state["n"] += 1

            for dgi, (t0, gn, r) in enumerate(dgroups):
                ensure_d(dgi * 8 + 1)
                dma("sp", Ug[:, :, 0:gn], UT.ap[:, t0:t0 + gn].rearrange("(kc p) t -> p kc t", p=P), UT.bufs(t0, gn), [Ug_b])
                gmts = [(o, min(512, gn - o)) for o in range(0, gn, 512)]
                for fc4 in range(4):
                    didx = dgi * 8 + fc4
                    ensure_d(didx + 2)
                    wt_, wb_ = wall[didx % 4]
                    for sub in range(4):
                        fcn = fc4 * 4 + sub
                        for (mo, mn) in gmts:
                            gt, gtb = gts[rr["ev"] % 3]
                            m1, m1b = m1s[rr["ev"] % 2]
                            m2, m2b = m2s[rr["ev"] % 2]
                            m3, m3b = m3s[rr["ev"] % 2]
                            rr["ev"] += 1
                            dma("sp", gt[:, :, 0:mn],
                                GT.ap[:, t0 + mo:t0 + mo + mn].rearrange("(b f) t -> f b t", b=3)[fcn * P:(fcn + 1) * P],
                                GT.bufs(t0 + mo, mn), [gtb])
                            bks = []
                            for br, (k0, k1) in enumerate(((0, 4), (4, 12), (12, 16))):
                                bk = rr["ps"] % 6
                                rr["ps"] += 1
                                bks.append(bk)
                                for kc in range(k0, k1):
                                    mm(ps[bk][:, 0:mn], wt_[:, kc, sub * P:(sub + 1) * P], Ug[:, kc, mo:mo + mn],
                                       kc == k0, kc == k1 - 1, [wb_, Ug_b], [pb[bk]])
                            tt("dve", m1[:, 0:mn], ps[bks[0]][:, 0:mn], gt[:, 0, 0:mn], ALU.mult, [pb[bks[0]], gtb], [m1b])
                            tt("dve", m2[:, 0:mn], ps[bks[1]][:, 0:mn], gt[:, 1, 0:mn], ALU.mult, [pb[bks[1]], gtb], [m2b])
                            tt("dve", m3[:, 0:mn], ps[bks[2]][:, 0:mn], gt[:, 2, 0:mn], ALU.mult, [pb[bks[2]], gtb], [m3b])
                            tt("pool", m1[:, 0:mn], m1[:, 0:mn], m2[:, 0:mn], ALU.add, [m1b, m2b], [m1b])
                            ybs = [yT_b[(mo + i) // P] for i in range(0, mn, P)]
                            tt("pool", yT[:, fcn, mo:mo + mn], m1[:, 0:mn], m3[:, 0:mn], ALU.add, [m1b, m3b], ybs)
                for nc4 in range(4):
                    didx = dgi * 8 + 4 + nc4
                    ensure_d(didx + 2)
                    wt_, wb_ = wall[didx % 4]
                    for tsb in range(gn // P):
                        tg = t0 // P + tsb
                        xr, xrb = xrs[rr["ev"] % 3]
                        xo, xob = xos[rr["ev"] % 3]
                        rr["ev"] += 1
                        dma("sp", xr, Xcur[tg * P:(tg + 1) * P, nc4 * 512:(nc4 + 1) * 512], [Xcur_b[tg]], [xrb])
                        bk = rr["ps"] % 6
                        rr["ps"] += 1
                        for fc in range(KC):
                            mm(ps[bk][:, :], yT[:, fc, tsb * P:(tsb + 1) * P], wt_[:, fc, :], fc == 0, fc == KC - 1,
                               [wb_, yT_b[tsb]], [pb[bk]])
                        tt("dve", xo, ps[bk][:, :], gbc[r][0][:, nc4 * 512:(nc4 + 1) * 512], ALU.mult, [pb[bk], gbc[r][1]], [xob])
                        tt("pool", xo, xo, xr, ALU.add, [xob, xrb], [xob])
                        dma("pool", Xnext[tg * P:(tg + 1) * P, nc4 * 512:(nc4 + 1) * 512], xo, [xob], [Xnext_b[tg]])
            A.release(mk)
            if not upd_ctx and not is_final:
                pass
            Xcur, Xcur_b = Xnext, Xnext_b

        finals = []
        if is_final:
            mk = A.mark()
            fw, fw_b = A.alloc("fw", (D,), F32)
            xts = [A.alloc("fx%d" % i, (D,), F32) for i in range(2)]
            xns = [A.alloc("fn%d" % i, (D,), F32) for i in range(2)]
            junk, junk_b = A.alloc("fjunk", (D,), BF16)
            dma("sp", fw, fnw, [], [fw_b])
            for tl in range(L // P):
                tg = 2 + tl
                xt_, xtb_ = xts[tl % 2]
                xn_, xnb_ = xns[tl % 2]
                sst, sstb = stat[1 + tl % 2]
                dma("sp", xt_, Xcur[tg * P:(tg + 1) * P, :], [Xcur_b[tg]], [xtb_])
                act(junk, xt_, AF.Square, [xtb_], [junk_b, sstb], accum=sst[:, 0:1])
                ts("dve", sst[:, 1:2], sst[:, 0:1], 1.0 / D, ALU.mult, [sstb], [sstb], s2=EPS, op1=ALU.add)
                tt("pool", sst[:, 2:3], sst[:, 1:2], nhalf[:, 0:1], ALU.pow, [sstb, nhalf_b], [sstb])
                stt(xn_, xt_, sst[:, 2:3], fw, ALU.mult, ALU.mult, [xtb_, sstb, fw_b], [xnb_])
                finals.append(dma("pool", yout[tl * P:(tl + 1) * P, :], xn_, [xnb_], [yo_b[tl]]))
            A.release(mk)
        else:
            for b in xo_b:
                if b.w is not None:
                    finals.append(b.w)
        if S.dead:
            finals = S.dead_tail
        counts = S.finalize(final_waits=_dedupe([f for f in finals if f is not None]))
    return nc, counts


def _prep_shared(inputs, l0, nl):
    f = lambda a: np.ascontiguousarray(a, dtype=np.float32)
    sl = slice(l0, l0 + nl)
    colT = lambda a, n: f(a.reshape(a.shape[0], n, P).transpose(0, 2, 1))
    cosb, sinb = _rope_tables(HD)
    cosc, sinc = _rope_tables(64)
    sh = {
        "norm_wT": colT(inputs["norm_w"][sl], KC),
        "w_ada": f(inputs["w_ada"][sl]),
        "b_adaT": colT(inputs["b_ada"][sl], 48),
        "w_in": f(inputs["w_in"][sl]),
        "b_gateT": colT(inputs["b_gate"][sl], 48),
        "biasA": _bias_tables(f(inputs["rpb"][sl])),
        "qnw": f(np.broadcast_to(inputs["q_norm_w"][sl][:, None, :], (nl, P, HD))),
        "knw": f(np.broadcast_to(inputs["k_norm_w"][sl][:, None, :], (nl, P, HD))),
        "lamv": f(np.broadcast_to(np.stack([inputs["lam_q1"][sl], inputs["lam_k1"][sl], inputs["lam_q2"][sl],
                                            inputs["lam_k2"][sl]], axis=1)[:, None], (nl, P, 4, 64))),
        "sublnT": f(inputs["subln_w"][sl][:, :, None]),
        "w_bo": f(np.concatenate([inputs["w_bo_a"][sl], inputs["w_bo_b"][sl], inputs["w_bo_c"][sl]], axis=1)),
        "w_out": f(inputs["w_out"][sl]),
        "fnw": f(np.broadcast_to(inputs["final_norm_w"][None, :], (P, D))),
        "cosb": cosb, "sinb": sinb, "cosc": cosc, "sinc": sinc,
        "ident": np.eye(P, dtype=np.float32),
    }
    return sh


def _cc(inputs, b):
    cc = np.stack([inputs["c"][b], inputs["c_ctx"]], axis=0).astype(np.float32)
    return np.ascontiguousarray(cc.reshape(2, KC, P).transpose(2, 1, 0))


_PROG_CACHE = {}


def _get_prog(nl, first, is_final):
    key = (nl, first, is_final)
    if key not in _PROG_CACHE:
        _PROG_CACHE[key] = build_program(nl, first, is_final)[0]
    return _PROG_CACHE[key]


FUSED = False


def kernel(**inputs):
    inputs = {k: np.asarray(v) for k, v in inputs.items()}
    B = inputs["x"].shape[0]
    xs = [np.ascontiguousarray(np.concatenate([inputs["ctx"][b], inputs["x"][b]], axis=0), dtype=np.float32) for b in range(B)]
    ccs = [_cc(inputs, b) for b in range(B)]
    if FUSED:
        plan = [(0, DEPTH)]
    else:
        plan = [(l, 1) for l in range(DEPTH)]
    out = None
    for (l0, nl) in plan:
        is_final = (l0 + nl == DEPTH)
        nc = _get_prog(nl, l0, is_final)
        sh = _prep_shared(inputs, l0, nl)
        in_maps = []
        for b in range(B):
            m = dict(sh)
            m["xin"] = xs[b]
            m["ccT"] = ccs[b]
            in_maps.append(m)
        res = run_bass_kernel_spmd(nc, in_maps, core_ids=list(range(B)))
        if is_final:
            out = np.stack([np.asarray(res.results[b]["y"]) for b in range(B)], axis=0).astype(np.float32)
        else:
            xs = [np.ascontiguousarray(np.asarray(res.results[b]["xout"]), dtype=np.float32) for b in range(B)]
    return out
```

```python
import math
from contextlib import ExitStack

import numpy as np
import concourse.bass as bass
import concourse.mybir as mybir
from concourse.bass_utils import run_bass_kernel_spmd

F32 = mybir.dt.float32
BF16 = mybir.dt.bfloat16
U8 = mybir.dt.uint8
AF = mybir.ActivationFunctionType
ALU = mybir.AluOpType
AX = mybir.AxisListType

P = 128
D = 2048
KC = 16
LC = 256
L = 4096
T = LC + L
NT = T // P
GRID_W = 64
HD = 128
N_IN = 12800
EPS = 1e-6
DEPTH = 4
NB_A = 21
MT = [(0, 256)] + [(256 + 512 * i, 512) for i in range(8)]

ENGS = ("pe", "act", "dve", "pool", "sp")


class Buf:
    __slots__ = ("name", "w", "r", "excl")

    def __init__(self, name, fence=(), excl=False):
        self.name = name
        self.w = None
        self.r = list(fence)
        self.excl = excl


class Ev:
    __slots__ = ("eng", "idx", "dsem", "dval")

    def __init__(self, eng, idx, dsem=None, dval=None):
        self.eng, self.idx, self.dsem, self.dval = eng, idx, dsem, dval


def _dedupe(evs):
    best = {}
    for d in evs:
        k = ("d",) + d.dsem if d.dsem is not None else ("p", d.eng)
        v = d.dval if d.dsem is not None else d.idx
        if k not in best or v > best[k][0]:
            best[k] = (v, d)
    return [x[1] for x in best.values()]


class Sched:
    def __init__(self, nc, n_dma_sems=10):
        self.nc = nc
        self.ops = {e: [] for e in ENGS}
        self.n_dma_sems = n_dma_sems
        self.dma_rr = {e: 0 for e in ENGS}
        self.dma_cnt = {}
        self.dma_last = {}
        self.fence = []
        self.dead = False
        self.dead_tail = None

    def kill(self):
        if not self.dead:
            self.dead_tail = self.tail_events()
            self.dead = True

    def _deps(self, reads, writes):
        deps = []
        raw = set()
        for b in reads:
            if b.w is not None:
                deps.append(b.w)
                raw.add(id(b.w))
            if b.excl:
                deps.extend(b.r)
        for b in writes:
            if b.w is not None:
                deps.append(b.w)
            deps.extend(b.r)
        self._raw = raw
        return deps


    def op(self, eng, fn, reads=(), writes=()):
        if self.dead:
            return None
        deps = self._deps(reads, writes)
        ev = Ev(eng, len(self.ops[eng]))
        self.ops[eng].append((deps, fn, ev, False, self._raw))
        for b in reads:
            b.r = [x for x in b.r if not (x.dsem is None and x.eng == eng)]
            b.r.append(ev)
        for b in writes:
            b.w = ev
            b.r = []
        return ev

    def dma(self, eng, fn, reads=(), writes=()):
        if self.dead:
            return None
        deps = self._deps(reads, writes)
        slot = self.dma_rr[eng] % self.n_dma_sems
        self.dma_rr[eng] += 1
        key = (eng, slot)
        cnt = self.dma_cnt.get(key, 0) + 1
        self.dma_cnt[key] = cnt
        prev = self.dma_last.get(key)
        if prev is not None:
            deps.append(prev)
        ev = Ev(eng, len(self.ops[eng]), dsem=key, dval=16 * cnt)
        self.dma_last[key] = ev
        self.ops[eng].append((deps, fn, ev, True, self._raw))
        for b in reads:
            b.r = _dedupe(b.r + [ev])
        for b in writes:
            b.w = ev
            b.r = []
        return ev

    def tail_events(self):
        evs = []
        for e in ENGS:
            for deps, fn, ev, is_dma, raw in reversed(self.ops[e]):
                if not is_dma:
                    evs.append(ev)
                    break
        evs.extend(self.dma_last.values())
        return evs

    def finalize(self, final_waits=()):
        nc = self.nc
        needed = {e: set() for e in ENGS}
        def same_needed(e, d, is_dma, raw):
            if e == "pe" or e == "sp":
                return False
            return is_dma or (id(d) in raw)

        for e in ENGS:
            for deps, fn, ev, is_dma, raw in self.ops[e]:
                for d in deps:
                    if d.dsem is None and (d.eng != e or same_needed(e, d, is_dma, raw)):
                        needed[d.eng].add(d.idx)
        for d in final_waits:
            if d.dsem is None:
                needed[d.eng].add(d.idx)
        val = {e: {} for e in ENGS}
        for e in ENGS:
            for c, i in enumerate(sorted(needed[e])):
                val[e][i] = c + 1
        with ExitStack() as stack:
            psem = {e: stack.enter_context(nc.semaphore("prog_" + e)) for e in ENGS}
            dsem = {key: stack.enter_context(nc.semaphore("dma_%s_%d" % key)) for key in self.dma_cnt}
            block = stack.enter_context(nc.Block())

            def make_body(e, extra_final):
                def body(engine):
                    seen = {}
                    for deps, fn, ev, is_dma, raw in self.ops[e]:
                        want = {}
                        for d in deps:
                            if d.dsem is not None:
                                k, v = ("d",) + d.dsem, d.dval
                            else:
                                if d.eng == e and not same_needed(e, d, is_dma, raw):
                                    continue
                                k, v = ("p", d.eng), val[d.eng][d.idx]
                            if seen.get(k, 0) >= v:
                                continue
                            if want.get(k, 0) < v:
                                want[k] = v
                        for k, v in want.items():
                            sem = dsem[k[1:]] if k[0] == "d" else psem[k[1]]
                            engine.wait_ge(sem, v)
                            seen[k] = v
                        ins = fn(engine)
                        if is_dma:
                            ins.then_inc(dsem[ev.dsem], 16)
                        elif ev.idx in val[e]:
                            ins.then_inc(psem[e], 1)
                    for d in extra_final:
                        if d.dsem is not None:
                            engine.wait_ge(dsem[d.dsem], d.dval)
                        else:
                            engine.wait_ge(psem[d.eng], val[d.eng][d.idx])
                return body

            block.tensor(make_body("pe", ()))
            block.scalar(make_body("act", ()))
            block.vector(make_body("dve", ()))
            block.gpsimd(make_body("pool", ()))
            block.sync(make_body("sp", tuple(final_waits)))
        return {e: len(self.ops[e]) for e in ENGS}


_DTSZ = {F32: 4, BF16: 2, U8: 1}


class Arena:
    def __init__(self, S, tensor, size):
        self.S, self.t, self.size = S, tensor, size
        self.off = 0
        self.live = []

    def alloc(self, name, free_shape, dtype):
        n = 1
        for s in free_shape:
            n *= s
        nbytes = n * _DTSZ[dtype]
        off = (self.off + 63) // 64 * 64
        assert off + nbytes <= self.size, "SBUF arena overflow at %s: need %d have %d" % (name, off + nbytes, self.size)
        ap = self.t[:, off:off + nbytes]
        if dtype != U8:
            ap = ap.bitcast(dtype)
        if len(free_shape) == 2:
            ap = ap.rearrange("p (a b) -> p a b", a=free_shape[0])
        elif len(free_shape) == 3:
            ap = ap.rearrange("p (a b c) -> p a b c", a=free_shape[0], b=free_shape[1])
        self.off = off + nbytes
        b = Buf(name, self.S.fence)
        self.live.append((self.off, b))
        return ap, b

    def extra_buf(self, name):
        b = Buf(name, self.S.fence)
        self.live.append((self.off, b))
        return b

    def mark(self):
        return self.off

    def release(self, mark):
        evs = list(self.S.fence)
        keep = []
        for off, b in self.live:
            if off > mark:
                if b.w is not None:
                    evs.append(b.w)
                evs.extend(b.r)
            else:
                keep.append((off, b))
        self.live = keep
        self.S.fence = _dedupe(evs)
        self.off = mark


class DramT:
    def __init__(self, ap, name, tok_axis):
        self.ap = ap
        self.tok_axis = tok_axis
        self.b = [Buf("%s_%d" % (name, i)) for i in range(len(MT))]

    def bufs(self, tok0, n):
        return [self.b[i] for i, (t0, tn) in enumerate(MT) if t0 < tok0 + n and tok0 < t0 + tn]


def _rope_tables(rot_dim):
    pos = np.arange(L)
    row = (pos // GRID_W).astype(np.float32)
    col = (pos % GRID_W).astype(np.float32)
    n = rot_dim // 4
    inv_freq = (10000.0 ** (-np.arange(n, dtype=np.float32) / n)).astype(np.float32)
    ang = np.concatenate([row[:, None] * inv_freq, col[:, None] * inv_freq], axis=-1).astype(np.float32)
    cos, sin = np.cos(ang).astype(np.float32), np.sin(ang).astype(np.float32)
    f = lambda a: np.ascontiguousarray(a.reshape(L // P, P, -1).transpose(1, 0, 2))
    return f(cos), f(sin)


def _natten_plan():
    rows = L // GRID_W
    kh, win_w = 8, 16
    r = np.arange(rows)
    row_start = np.clip(r - kh // 2, 0, rows - kh)
    cidx = np.arange(GRID_W)
    col_start = np.clip(cidx - win_w // 2, 0, GRID_W - win_w)
    plan = []
    case_base = {}
    tile_defs = []
    for j in range(rows // 2):
        qrows = [2 * j, 2 * j + 1]
        krows = sorted(set(int(x) for qr in qrows for x in range(row_start[qr], row_start[qr] + kh)))
        jjs = sorted(set(kr // 2 for kr in krows))
        case = j if j in (0, 1, rows // 2 - 2, rows // 2 - 1) else "mid"
        new_case = case not in case_base
        if new_case:
            case_base[case] = len(tile_defs)
        ent = []
        for wi, jj in enumerate(jjs):
            kr = np.repeat(np.array([2 * jj, 2 * jj + 1]), GRID_W)
            kc = np.tile(cidx, 2)
            qr = np.repeat(np.array(qrows), GRID_W)
            qc = np.tile(cidx, 2)
            in_rows = (kr[:, None] >= row_start[qr][None, :]) & (kr[:, None] < row_start[qr][None, :] + kh)
            in_cols = (kc[:, None] >= col_start[qc][None, :]) & (kc[:, None] < col_start[qc][None, :] + win_w)
            mask = in_rows & in_cols
            dr = np.clip(kr[:, None] - qr[None, :] + 7, 0, 14)
            dc = np.clip(kc[:, None] - qc[None, :], -15, 15) + 15
            if new_case:
                tile_defs.append((dr, dc, mask))
            else:
                d0, d1, m0 = tile_defs[case_base[case] + wi]
                assert (m0 == mask).all() and ((d0 * m0) == (dr * mask)).all() and ((d1 * m0) == (dc * mask)).all()
            ent.append((jj, case_base[case] + wi))
        plan.append(ent)
    return plan, tile_defs


_NPLAN, _NTILES = _natten_plan()
assert len(_NTILES) == NB_A, len(_NTILES)


def _bias_tables(rpb):
    nl = rpb.shape[0]
    out = np.empty((nl, P, 4, NB_A, P), np.float32)
    for i, (dr, dc, mask) in enumerate(_NTILES):
        g = rpb[:, :, dr, dc]
        g = np.where(mask[None, None], g, np.float32(-30000.0))
        out[:, :, :, i, :] = g.transpose(0, 2, 1, 3)
    return out


class _Stop(Exception):
    pass


def build_program(nl, first_layer, is_final, dbg=False, stop=None):
    nc = bass.Bass("TRN2", target_bir_lowering=False)
    dram_in = lambda name, shape, dt=F32: nc.dram_tensor(name, list(shape), dt, kind="ExternalInput").ap()
    dram_out = lambda name, shape, dt=F32: nc.dram_tensor(name, list(shape), dt, kind="ExternalOutput").ap()

    def dram_scr(name, shape, dt):
        if dbg:
            return nc.dram_tensor(name, list(shape), dt, kind="ExternalOutput").ap()
        return nc.dram_tensor(name, list(shape), dt).ap()

    xin = dram_in("xin", [T, D])
    ccT = dram_in("ccT", [P, KC, 2])
    norm_wT = dram_in("norm_wT", [nl, P, KC])
    w_ada = dram_in("w_ada", [nl, D, 3 * D])
    b_adaT = dram_in("b_adaT", [nl, P, 48])
    w_in = dram_in("w_in", [nl, D, N_IN])
    b_gateT = dram_in("b_gateT", [nl, P, 48])
    biasA = dram_in("biasA", [nl, P, 4, NB_A, P])
    qnw = dram_in("qnw", [nl, P, HD])
    knw = dram_in("knw", [nl, P, HD])
    lamv = dram_in("lamv", [nl, P, 4, 64])
    sublnT = dram_in("sublnT", [nl, P, 1])
    w_bo = dram_in("w_bo", [nl, D, D])
    w_out = dram_in("w_out", [nl, D, D])
    fnw = dram_in("fnw", [P, D])
    cosb_d = dram_in("cosb", [P, 32, 64])
    sinb_d = dram_in("sinb", [P, 32, 64])
    cosc_d = dram_in("cosc", [P, 32, 32])
    sinc_d = dram_in("sinc", [P, 32, 32])
    ident_d = dram_in("ident", [P, P])
    if is_final:
        yout = dram_out("y", [L, D])
    else:
        xout = dram_out("xout", [T, D])

    Xs = dram_scr("Xs", [T, D], F32)
    QAT = DramT(dram_scr("QAT", [512, T], BF16), "QAT", 1)
    KAT = DramT(dram_scr("KAT", [512, T], BF16), "KAT", 1)
    VA = DramT(dram_scr("VA", [T, 512], BF16), "VA", 0)
    SZT = DramT(dram_scr("SZT", [2048, T], BF16), "SZT", 1)
    QBT = DramT(dram_scr("QBT", [1024, T], BF16), "QBT", 1)
    KBT = DramT(dram_scr("KBT", [256, T], BF16), "KBT", 1)
    VB = DramT(dram_scr("VB", [T, 256], BF16), "VB", 0)
    QCT = DramT(dram_scr("QCT", [512, T], BF16), "QCT", 1)
    KCT = DramT(dram_scr("KCT", [512, T], BF16), "KCT", 1)
    VC = DramT(dram_scr("VC", [T, 512], BF16), "VC", 0)
    GT = DramT(dram_scr("GT", [6144, T], BF16), "GT", 1)
    UT = DramT(dram_scr("UT", [2048, T], BF16), "UT", 1)
    xs_b = [Buf("Xs_%d" % i) for i in range(NT)]
    xo_b = [Buf("xo_%d" % i) for i in range(NT)]
    yo_b = [Buf("y_%d" % i) for i in range(NT)]

    ARENA = 199 * 1024
    with ExitStack() as st:
        arena_t = st.enter_context(nc.sbuf_tensor("arena", [P, ARENA], U8))
        small_t = st.enter_context(nc.sbuf_tensor("small", [P, 5 * 1024], U8))
        ps = [st.enter_context(nc.psum_tensor("ps%d" % i, [P, 512], F32)) for i in range(8)]
        pb = [Buf("ps%d" % i, excl=True) for i in range(8)]
        S = Sched(nc)
        A = Arena(S, arena_t, ARENA)
        SM = Arena(S, small_t, 5 * 1024)

        def mm(out, lhsT, rhs, start, stop, R, W):
            return S.op("pe", lambda e: e.matmul(out, lhsT, rhs, start=start, stop=stop), R, W)

        def tr(out, in_, ident, R, W):
            return S.op("pe", lambda e: e.transpose(out, in_, ident), R, W)

        def act(out, in_, func, R, W, bias=None, scale=None, accum=None):
            kw = {}
            if bias is not None:
                kw["bias"] = bias
            if scale is not None:
                kw["scale"] = scale
            if accum is not None:
                kw["accum_out"] = accum
            return S.op("act", lambda e: e.activation(out=out, in_=in_, func=func, **kw), R, W)

        def tt(eng, out, a, b, op, R, W):
            return S.op(eng, lambda e: e.tensor_tensor(out=out, in0=a, in1=b, op=op), R, W)

        def ts(eng, out, a, s1, op0, R, W, s2=None, op1=None):
            if op1 is None:
                return S.op(eng, lambda e: e.tensor_scalar(out=out, in0=a, scalar1=s1, scalar2=None, op0=op0), R, W)
            return S.op(eng, lambda e: e.tensor_scalar(out=out, in0=a, scalar1=s1, scalar2=s2, op0=op0, op1=op1), R, W)

        def stt(out, a, scalar, b, op0, op1, R, W):
            return S.op("dve", lambda e: e.scalar_tensor_tensor(out=out, in0=a, scalar=scalar, in1=b, op0=op0, op1=op1), R, W)

        def cp(eng, out, in_, R, W):
            if eng == "act":
                return S.op("act", lambda e: e.activation(out=out, in_=in_, func=AF.Copy), R, W)
            return S.op(eng, lambda e: e.tensor_copy(out=out, in_=in_), R, W)

        def dma(q, out, in_, R, W):
            return S.dma(q, lambda e: e.dma_start(out=out, in_=in_), R, W)

        def memset(eng, ap, val, W):
            return S.op(eng, lambda e: e.memset(ap, val), (), W)

        def bc(ap, shape):
            return ap.broadcast_to(list(shape))

        rr = {"ps": 0, "ev": 0}

        ident_f, ident_f_b = SM.alloc("ident_f", (P,), F32)
        ident_b, ident_b_b = SM.alloc("ident_b", (P,), BF16)
        ones_b, ones_b_b = SM.alloc("ones_b", (P,), BF16)
        ones_f, ones_f_b = SM.alloc("ones_f", (P,), F32)
        sc, sc_b = SM.alloc("sc", (KC, 2), F32)
        nhalf, nhalf_b = SM.alloc("nhalf", (8,), F32)
        modT, modT_b = SM.alloc("modT", (48, 2), F32)
        gcol, gcol_b = SM.alloc("gcol", (KC, 2), F32)
        nwT, nwT_b = SM.alloc("nwT", (KC,), F32)
        badaT, badaT_b = SM.alloc("badaT", (48,), F32)
        bgT, bgT_b = SM.alloc("bgT", (48,), F32)
        lamc, lamc_b = SM.alloc("lamc", (8,), F32)
        lamin, lamin_b = SM.alloc("lamin", (4, 64), F32)
        sublc, sublc_b = SM.alloc("sublc", (2,), F32)
        stat = [SM.alloc("stat%d" % i, (16,), F32) for i in range(6)]

        dma("sp", ident_f, ident_d, [], [ident_f_b])
        dma("pool", ident_b, ident_d, [], [ident_b_b])
        memset("dve", ones_b, 1.0, [ones_b_b])
        memset("dve", ones_f, 1.0, [ones_f_b])
        memset("dve", nhalf, -0.5, [nhalf_b])
        dma("sp", sc, ccT, [], [sc_b])
        act(sc, sc, AF.Silu, [sc_b], [sc_b])

        Xcur = xin
        Xcur_b = [Buf("xin_%d" % i) for i in range(NT)]

        def ckpt(name):
            if stop == name:
                S.kill()

        for li in range(nl):
            labs = first_layer + li
            last_mod_layer = (labs == DEPTH - 1)
            upd_ctx = not last_mod_layer
            lam_init = 0.8 - 0.6 * math.exp(-0.3 * labs)

            mk = A.mark()
            dma("sp", nwT, norm_wT[li], [], [nwT_b])
            dma("sp", badaT, b_adaT[li], [], [badaT_b])
            dma("sp", bgT, b_gateT[li], [], [bgT_b])
            dma("sp", lamin, lamv[li], [], [lamin_b])
            dma("sp", sublc[:, 0:1], sublnT[li], [], [sublc_b])
            wa = [A.alloc("wa%d" % i, (KC, 512), F32) for i in range(2)]
            wav = w_ada[li].rearrange("(kc p) n -> p kc n", p=P)
            pA, pAb = ps[0], pb[0]
            for c in range(12):
                wt_, wb_ = wa[c % 2]
                dma("sp", wt_, wav[:, :, c * 512:(c + 1) * 512], [], [wb_])
                for sub in range(4):
                    n = c * 4 + sub
                    for kc in range(KC):
                        mm(pA[:, n * 2:(n + 1) * 2], wt_[:, kc, sub * P:(sub + 1) * P], sc[:, kc, :],
                           kc == 0, kc == KC - 1, [wb_, sc_b], [pAb])
            tt("dve", modT, pA[:, 0:96].rearrange("p (a b) -> p a b", b=2),
               bc(badaT.rearrange("p (a b) -> p a b", b=1), (P, 48, 2)), ALU.add, [pAb, badaT_b], [modT_b])
            stt(gcol, modT[:, 16:32, :], 1.0, bc(nwT.rearrange("p (a b) -> p a b", b=1), (P, KC, 2)),
                ALU.add, ALU.mult, [modT_b, nwT_b], [gcol_b])
            st0, st0b = stat[0]
            tt("dve", lamin[:, 0, :], lamin[:, 0, :], lamin[:, 1, :], ALU.mult, [lamin_b], [lamin_b])
            tt("dve", lamin[:, 2, :], lamin[:, 2, :], lamin[:, 3, :], ALU.mult, [lamin_b], [lamin_b])
            S.op("dve", lambda e, o=st0[:, 0:1], i=lamin[:, 0, :]: e.tensor_reduce(out=o, in_=i, axis=AX.X, op=ALU.add), [lamin_b], [st0b])
            S.op("dve", lambda e, o=st0[:, 1:2], i=lamin[:, 2, :]: e.tensor_reduce(out=o, in_=i, axis=AX.X, op=ALU.add), [lamin_b], [st0b])
            act(st0[:, 2:4], st0[:, 0:2], AF.Exp, [st0b], [st0b])
            stt(lamc[:, 0:1], st0[:, 3:4], -lam_init, st0[:, 2:3], ALU.add, ALU.subtract, [st0b], [lamc_b])
            ts("dve", lamc[:, 1:2], sublc[:, 0:1], 1.0 - lam_init, ALU.mult, [sublc_b], [lamc_b])
            A.release(mk)
            ckpt("ada")

            mk = A.mark()
            HTN = 1792
            hT, _ = A.alloc("hT", (KC, HTN), BF16)
            hT_b = [A.extra_buf("hT_%d" % i) for i in range(HTN // P)]
            NWT = 3
            wts = [A.alloc("wt%d" % i, (KC, 512), BF16) for i in range(NWT)]
            xts = [A.alloc("xt%d" % i, (D,), F32) for i in range(2)]
            xn, xn_b = A.alloc("xn", (D,), F32)
            junk, junk_b = A.alloc("junk", (D,), BF16)
            stF = [A.alloc("stF%d" % i, (4, 512), BF16) for i in range(2)]
            stV = [A.alloc("stV%d" % i, (512,), BF16) for i in range(2)]
            stT = [A.alloc("stT%d" % i, (4, 512), BF16) for i in range(3)]
            sq, sq_b = A.alloc("sq", (512,), F32)
            yb = [A.alloc("yb%d" % i, (512,), F32) for i in range(2)]
            tmp1 = [A.alloc("tmpA%d" % i, (256,), F32) for i in range(2)]
            tmp2 = [A.alloc("tmpB%d" % i, (256,), F32) for i in range(2)]
            tmp3 = [A.alloc("tmpC%d" % i, (256,), F32) for i in range(2)]
            tmp4 = [A.alloc("tmpD%d" % i, (256,), F32) for i in range(2)]
            ob = [A.alloc("ob%d" % i, (512,), BF16) for i in range(4)]
            ob_state = {"n": 0}
            cosb, cosb_b = A.alloc("cosb", (32, 64), F32)
            sinb, sinb_b = A.alloc("sinb", (32, 64), F32)
            cosc, cosc_b = A.alloc("cosc", (32, 32), F32)
            sinc, sinc_b = A.alloc("sinc", (32, 32), F32)
            qnw_t, qnw_b = A.alloc("qnw", (HD,), F32)
            knw_t, knw_b = A.alloc("knw", (HD,), F32)
            dma("sp", cosb, cosb_d, [], [cosb_b])
            dma("sp", sinb, sinb_d, [], [sinb_b])
            dma("sp", cosc, cosc_d, [], [cosc_b])
            dma("sp", sinc, sinc_d, [], [sinc_b])
            dma("sp", qnw_t, qnw[li], [], [qnw_b])
            dma("sp", knw_t, knw[li], [], [knw_b])
            winv = w_in[li].rearrange("(kc p) n -> p kc n", p=P)

            CH = {}
            CH[0] = ("Fcopy", QAT, 0)
            CH[1] = ("Fcopy", KAT, 0)
            CH[2] = ("Tv", [(VA, 0, 0, 512)], None)
            CH[3] = ("Fsilu", SZT, 0)
            CH[4] = ("Tqk", [("b", QBT, 0, 0, 4, qnw_t, qnw_b)], None)
            CH[5] = ("Tqk", [("b", QBT, 512, 0, 4, qnw_t, qnw_b)], None)
            CH[6] = ("Tmix", None, None)
            CH[7] = ("Fsilu", SZT, 512)
            CH[8] = ("Fsilu", SZT, 1024)
            CH[9] = ("Tqk", [("c", QCT, 0, 0, 4, None, None)], None)
            CH[10] = ("Tqk", [("c", KCT, 0, 0, 4, None, None)], None)
            CH[11] = ("Tv", [(VC, 0, 0, 512)], None)
            CH[12] = ("Fsilu", SZT, 1536)
            for i in range(12):
                CH[13 + i] = ("Fgate", GT, 512 * i)

            groups = [MT[0:4], MT[4:7], MT[7:9]]
            xcount = 0
            pcount = 0
            w_items = [(gi_, c_) for gi_ in range(len(groups)) for c_ in range(25)]
            w_state = {"n": 0}

            def ensure_w(upto):
                while w_state["n"] <= min(upto, len(w_items) - 1):
                    i_ = w_state["n"]
                    w_t, w_b = wts[i_ % NWT]
                    c_ = w_items[i_][1]
                    dma("pool", w_t, winv[:, :, c_ * 512:(c_ + 1) * 512], [], [w_b])
                    w_state["n"] += 1

            for gi, grp in enumerate(groups):
                ensure_w(gi * 25 + 1)
                g_tok0 = grp[0][0]
                g_n = sum(n for _, n in grp)
                for tl in range(g_n // P):
                    tg = (g_tok0 // P) + tl
                    r = 1 if tg < 2 else 0
                    xt_, xtb_ = xts[xcount % 2]
                    sst, sstb = stat[1 + xcount % 2]
                    xcount += 1
                    dma("sp", xt_, Xcur[tg * P:(tg + 1) * P, :], [Xcur_b[tg]], [xtb_])
                    act(junk, xt_, AF.Square, [xtb_], [junk_b, sstb], accum=sst[:, 0:1])
                    ts("dve", sst[:, 1:2], sst[:, 0:1], 1.0 / D, ALU.mult, [sstb], [sstb], s2=EPS, op1=ALU.add)
                    tt("pool", sst[:, 2:3], sst[:, 1:2], nhalf[:, 0:1], ALU.pow, [sstb, nhalf_b], [sstb])
                    ts("dve", xn, xt_, sst[:, 2:3], ALU.mult, [xtb_, sstb], [xn_b])
                    for q4 in range(4):
                        bk = 6 + (pcount % 2)
                        pcount += 1
                        for j in range(4):
                            kc = q4 * 4 + j
                            tr(ps[bk][:, j * P:(j + 1) * P], xn[:, kc * P:(kc + 1) * P], ident_f, [xn_b, ident_f_b], [pb[bk]])
                        for j in range(4):
                            kc = q4 * 4 + j
                            o_ = hT[:, kc, tl * P:(tl + 1) * P]
                            i_ = ps[bk][:, j * P:(j + 1) * P]
                            if bk == 6:
                                act(o_, i_, AF.Identity, [pb[bk], gcol_b, modT_b], [hT_b[tl]],
                                    bias=modT[:, kc, r:r + 1], scale=gcol[:, kc, r:r + 1])
                            else:
                                ts("dve", o_, i_, gcol[:, kc, r:r + 1], ALU.mult, [pb[bk], gcol_b, modT_b], [hT_b[tl]],
                                   s2=modT[:, kc, r:r + 1], op1=ALU.add)
                ckpt("ab_a%d" % gi)
                mts = []
                loc = 0
                for (t0, n) in grp:
                    mts.append((t0, n, loc))
                    loc += n
                for c in range(25):
                    ckpt("ab_g%dc%d" % (gi, c))
                    widx = gi * 25 + c
                    ensure_w(widx + 2)
                    wt_, wb_ = wts[widx % NWT]
                    kind = CH[c][0]
                    if kind[0] == "F":
                        dst, row0 = CH[c][1], CH[c][2]
                        for (t0, n, loc) in mts:
                            stg, stgb = stF[rr["ev"] % 2]
                            rr["ev"] += 1
                            hb = [hT_b[(loc + i) // P] for i in range(0, n, P)]
                            for sub in range(4):
                                bk = rr["ps"] % 6
                                rr["ps"] += 1
                                for kc in range(KC):
                                    mm(ps[bk][:, 0:n], wt_[:, kc, sub * P:(sub + 1) * P], hT[:, kc, loc:loc + n],
                                       kc == 0, kc == KC - 1, [wb_] + hb, [pb[bk]])
                                if kind == "Fcopy":
                                    cp("dve", stg[:, sub, 0:n], ps[bk][:, 0:n], [pb[bk]], [stgb])
                                elif kind == "Fsilu":
                                    act(stg[:, sub, 0:n], ps[bk][:, 0:n], AF.Silu, [pb[bk]], [stgb])
                                else:
                                    gidx = (row0 // P) + sub
                                    act(stg[:, sub, 0:n], ps[bk][:, 0:n], AF.Sigmoid, [pb[bk], bgT_b], [stgb],
                                        bias=bgT[:, gidx:gidx + 1])
                            dma("pool", dst.ap[row0:row0 + 512, t0:t0 + n].rearrange("(s p) t -> p s t", p=P),
                                stg[:, :, 0:n], [stgb], dst.bufs(t0, n))
                    else:
                        if kind == "Tv":
                            parts = [("v",) + CH[c][1][0]]
                        elif kind == "Tqk":
                            parts = CH[c][1]
                        else:
                            parts = [("b", KBT, 0, 0, 2, knw_t, knw_b), ("v", VB, 0, 256, 256)]
                        pend_tr = []
                        pc_state = {"n": pcount}
                        for (t0, n, loc) in mts:
                            stt_, sttb = stT[ob_state["n"] % 3]
                            for tsub in range(n // P):
                                tg = (t0 // P) + tsub
                                is_lat = tg >= 2
                                lt = tg - 2
                                bk = rr["ps"] % 6
                                rr["ps"] += 1
                                tl = (loc // P) + tsub
                                for kc in range(KC):
                                    mm(ps[bk][:, :], hT[:, kc, tl * P:(tl + 1) * P], wt_[:, kc, :],
                                       kc == 0, kc == KC - 1, [wb_, hT_b[tl]], [pb[bk]])
                                for part in parts:
                                    if part[0] == "v":
                                        _, dstv, dcol0, pcol0, ncol = part
                                        sv, svb = stV[rr["ev"] % 2]
                                        rr["ev"] += 1
                                        cp("act", sv[:, 0:ncol], ps[bk][:, pcol0:pcol0 + ncol], [pb[bk]], [svb])
                                        dma("pool", dstv.ap[tg * P:(tg + 1) * P, dcol0:dcol0 + ncol], sv[:, 0:ncol],
                                            [svb], dstv.bufs(tg * P, P))
                                        continue
                                    typ, dstq, drow0, pcol0, nh, nw_t, nw_b = part
                                    ncol = nh * HD
                                    src = ps[bk][:, pcol0:pcol0 + ncol]
                                    o_, o_b = ob[ob_state["n"] % 4]
                                    ob_state["n"] += 1
                                    y_, y_b = yb[rr["ev"] % 2]
                                    t1, t1b = tmp1[rr["ev"] % 2]
                                    t2, t2b = tmp2[rr["ev"] % 2]
                                    t3, t3b = tmp3[rr["ev"] % 2]
                                    t4, t4b = tmp4[rr["ev"] % 2]
                                    rr["ev"] += 1
                                    if typ == "b":
                                        sst, sstb = stat[3 + rr["ev"] % 2]
                                        act(sq[:, 0:ncol], src, AF.Square, [pb[bk]], [sq_b])
                                        S.op("dve", lambda e, o=sst[:, 0:nh], i=sq[:, 0:ncol].rearrange("p (h d) -> p h d", d=HD):
                                             e.tensor_reduce(out=o, in_=i, axis=AX.X, op=ALU.add), [sq_b], [sstb])
                                        ts("dve", sst[:, 4:4 + nh], sst[:, 0:nh], 1.0 / HD, ALU.mult, [sstb], [sstb], s2=EPS, op1=ALU.add)
                                        tt("pool", sst[:, 8:8 + nh], sst[:, 4:4 + nh], nhalf[:, 0:nh], ALU.pow, [sstb, nhalf_b], [sstb])
                                        for hh in range(nh):
                                            stt(y_[:, hh * HD:(hh + 1) * HD], src[:, hh * HD:(hh + 1) * HD], sst[:, 8 + hh:9 + hh],
                                                nw_t, ALU.mult, ALU.mult, [pb[bk], sstb, nw_b], [y_b])
                                        ysrc, ysrc_b = y_[:, 0:ncol], y_b
                                        ngrp, half = nh, 64
                                        cs_t, cs_b, sn_t, sn_b = cosb, cosb_b, sinb, sinb_b
                                    else:
                                        ysrc, ysrc_b = src, pb[bk]
                                        ngrp, half = nh * 2, 32
                                        cs_t, cs_b, sn_t, sn_b = cosc, cosc_b, sinc, sinc_b
                                    if is_lat:
                                        yv = ysrc.rearrange("p (g t d) -> p g t d", t=2, d=half)
                                        ov = o_[:, 0:ncol].rearrange("p (g t d) -> p g t d", t=2, d=half)
                                        y1, y2 = yv[:, :, 0, :], yv[:, :, 1, :]
                                        cs = bc(cs_t[:, lt, :].rearrange("p (o d) -> p o d", o=1), (P, ngrp, half))
                                        sn = bc(sn_t[:, lt, :].rearrange("p (o d) -> p o d", o=1), (P, ngrp, half))
                                        nel = ngrp * half
                                        v3 = lambda t_: t_[:, 0:nel].rearrange("p (g d) -> p g d", d=half)
                                        tt("dve", v3(t1), y1, cs, ALU.mult, [ysrc_b, cs_b], [t1b])
                                        tt("dve", v3(t2), y2, sn, ALU.mult, [ysrc_b, sn_b], [t2b])
                                        tt("dve", v3(t3), y1, sn, ALU.mult, [ysrc_b, sn_b], [t3b])
                                        tt("dve", v3(t4), y2, cs, ALU.mult, [ysrc_b, cs_b], [t4b])
                                        tt("pool", ov[:, :, 0, :], v3(t1), v3(t2), ALU.subtract, [t1b, t2b], [o_b])
                                        tt("pool", ov[:, :, 1, :], v3(t3), v3(t4), ALU.add, [t3b, t4b], [o_b])
                                    else:
                                        cp("dve", o_[:, 0:ncol], ysrc, [ysrc_b], [o_b])
                                    def _flush(o_=o_, o_b=o_b, nh=nh, drow0=drow0, stt_=stt_, sttb=sttb, tsub=tsub,
                                               last=(tsub == n // P - 1), dstq=dstq, t0=t0, n=n):
                                        bkT = 6 + (pc_state["n"] % 2)
                                        pc_state["n"] += 1
                                        pT = ps[bkT][:].bitcast(BF16)
                                        for h in range(nh):
                                            tr(pT[:, h * P:(h + 1) * P], o_[:, h * P:(h + 1) * P], ident_b, [o_b, ident_b_b], [pb[bkT]])
                                        hoff = (drow0 % 512) // P
                                        cp("dve", stt_[:, hoff:hoff + nh, tsub * P:(tsub + 1) * P],
                                           pT[:, 0:nh * P].rearrange("p (h t) -> p h t", t=P), [pb[bkT]], [sttb])
                                        if last:
                                            dma("pool", dstq.ap[drow0:drow0 + nh * P, t0:t0 + n].rearrange("(s p) t -> p s t", p=P),
                                                stt_[:, hoff:hoff + nh, 0:n], [sttb], dstq.bufs(t0, n))
                                    pend_tr.append(_flush)
                                while len(pend_tr) > 2:
                                    pend_tr.pop(0)()
                        while pend_tr:
                            pend_tr.pop(0)()
            A.release(mk)
            ckpt("ab")

            q_blocks = [(256 + 512 * i, 512, True) for i in range(8)]
            if upd_ctx:
                q_blocks = q_blocks + [(0, 256, False)]

            def finish_block(o_bank, s_bank, n, szt, szb, ust, ustb, extra=None):
                rec, recb = extra
                S.op("dve", lambda e: e.reciprocal(out=rec[:, 0:n], in_=ps[s_bank][:, 0:n]), [pb[s_bank]], [recb])
                tt("dve", rec[:, 0:n], ps[o_bank][:, 0:n], rec[:, 0:n], ALU.mult, [pb[o_bank], recb], [recb])
                tt("pool", ust[:, 0:n], rec[:, 0:n], szt[:, 0:n], ALU.mult, [recb, szb], [ustb])

            mk = A.mark()
            sclA = HD ** -0.5
            KT_, KT_b = A.alloc("KTa", (4, T), BF16)
            V_, V_b = A.alloc("Va", (NT, 512), BF16)
            bA, bA_b = A.alloc("biasA", (4, NB_A, P), F32)
            Qs = [A.alloc("Qa%d" % i, (512,), BF16) for i in range(2)]
            SZs = [A.alloc("SZa%d" % i, (512,), BF16) for i in range(2)]
            Pt = [A.alloc("Pa%d" % i, (7, P), BF16) for i in range(3)]
            tmpS = [A.alloc("tSa%d" % i, (5, P), F32) for i in range(2)]
            recs = [A.alloc("reca%d" % i, (512,), F32) for i in range(2)]
            usts = [A.alloc("usta%d" % i, (512,), BF16) for i in range(2)]
            dma("sp", KT_, KAT.ap.rearrange("(h p) t -> p h t", p=P), KAT.b, [KT_b])
            dma("sp", V_, VA.ap.rearrange("(t p) c -> p t c", p=P), VA.b, [V_b])
            dma("sp", bA, biasA[li], [], [bA_b])
            blk = 0
            for h in range(4):
                for (q0, qn, is_lat) in q_blocks:
                    Qt, Qb = Qs[blk % 2]
                    SZ, SZb = SZs[blk % 2]
                    rc = recs[blk % 2]
                    ust, ustb = usts[blk % 2]
                    blk += 1
                    dma("sp", Qt[:, 0:qn], QAT.ap[h * P:(h + 1) * P, q0:q0 + qn], QAT.bufs(q0, qn), [Qb])
                    dma("sp", SZ[:, 0:qn], SZT.ap[h * P:(h + 1) * P, q0:q0 + qn], SZT.bufs(q0, qn), [SZb])
                    o_bank, s_bank = 4, 5
                    if is_lat:
                        for i in range(4):
                            j = (q0 - LC) // P + i
                            ent = _NPLAN[j]
                            nw = len(ent)
                            Pq, Pqb = Pt[rr["ev"] % 3]
                            tS, tSb = tmpS[rr["ev"] % 2]
                            rr["ev"] += 1
                            qcol = Qt[:, i * P:(i + 1) * P]
                            b0 = (rr["ps"] % 2) * 2
                            rr["ps"] += 1
                            for wi, (jj, bi) in enumerate(ent):
                                kt = 2 + jj
                                if wi < 4:
                                    o_ps = ps[b0][:, wi * P:(wi + 1) * P]
                                    bkk = b0
                                else:
                                    o_ps = ps[b0 + 1][:, 0:P]
                                    bkk = b0 + 1
                                mm(o_ps, KT_[:, h, kt * P:(kt + 1) * P], qcol, True, True, [KT_b, Qb], [pb[bkk]])
                            for ci in range(2):
                                mm(ps[b0 + 1][:, (1 + ci) * P:(2 + ci) * P], KT_[:, h, ci * P:(ci + 1) * P], qcol, True, True,
                                   [KT_b, Qb], [pb[b0 + 1]])
                            bi0 = ent[0][1]
                            assert [e[1] for e in ent] == list(range(bi0, bi0 + nw))
                            n4 = min(nw, 4)
                            stt(tS[:, 0:n4, :], ps[b0][:, 0:n4 * P].rearrange("p (a b) -> p a b", b=P), sclA,
                                bA[:, h, bi0:bi0 + n4, :], ALU.mult, ALU.add, [pb[b0], bA_b], [tSb])
                            if nw == 5:
                                stt(tS[:, 4, :], ps[b0 + 1][:, 0:P], sclA, bA[:, h, bi0 + 4, :], ALU.mult, ALU.add,
                                    [pb[b0 + 1], bA_b], [tSb])
                            act(Pq[:, 0:nw, :], tS[:, 0:nw, :], AF.Exp, [tSb], [Pqb])
                            act(Pq[:, 5:7, :], ps[b0 + 1][:, P:3 * P].rearrange("p (a b) -> p a b", b=P), AF.Exp,
                                [pb[b0 + 1]], [Pqb], scale=sclA)
                            tiles = [(2 + jj, wi) for wi, (jj, bi) in enumerate(ent)] + [(0, 5), (1, 6)]
                            for ti, (kt, pi) in enumerate(tiles):
                                mm(ps[o_bank][:, i * P:(i + 1) * P], V_[:, kt, h * P:(h + 1) * P], Pq[:, pi, :],
                                   ti == 0, ti == len(tiles) - 1, [V_b, Pqb], [pb[o_bank]])
                            for ti, (kt, pi) in enumerate(tiles):
                                mm(ps[s_bank][:, i * P:(i + 1) * P], ones_b, Pq[:, pi, :],
                                   ti == 0, ti == len(tiles) - 1, [ones_b_b, Pqb], [pb[s_bank]])
                    else:
                        Pq, Pqb = Pt[rr["ev"] % 3]
                        rr["ev"] += 1
                        b0 = (rr["ps"] % 2) * 2
                        rr["ps"] += 1
                        for ci in range(2):
                            mm(ps[b0][:, ci * 256:(ci + 1) * 256], KT_[:, h, ci * P:(ci + 1) * P], Qt[:, 0:256], True, True,
                               [KT_b, Qb], [pb[b0]])
                        Pv = Pq[:, 0:4, :].rearrange("p a b -> p (a b)")
                        act(Pv, ps[b0][:, 0:512], AF.Exp, [pb[b0]], [Pqb], scale=sclA)
                        for ci in range(2):
                            mm(ps[o_bank][:, 0:256], V_[:, ci, h * P:(h + 1) * P], Pv[:, ci * 256:(ci + 1) * 256],
                               ci == 0, ci == 1, [V_b, Pqb], [pb[o_bank]])
                        for ci in range(2):
                            mm(ps[s_bank][:, 0:256], ones_b, Pv[:, ci * 256:(ci + 1) * 256],
                               ci == 0, ci == 1, [ones_b_b, Pqb], [pb[s_bank]])
                    finish_block(o_bank, s_bank, qn, SZ, SZb, ust, ustb, extra=rc)
                    dma("pool", UT.ap[h * P:(h + 1) * P, q0:q0 + qn], ust[:, 0:qn], [ustb], UT.bufs(q0, qn))
            A.release(mk)
            ckpt("attA")

            mk = A.mark()
            sclB = HD ** -0.5
            KT_, KT_b = A.alloc("KTb", (2, T), BF16)
            V_, V_b = A.alloc("Vb", (NT, 256), BF16)
            Qs = [A.alloc("Qb%d" % i, (4, 512), BF16) for i in range(2)]
            SZs = [A.alloc("SZb%d" % i, (4, 512), BF16) for i in range(2)]
            Pt = [A.alloc("Pb%d" % i, (512,), BF16) for i in range(4)]
            recs = [A.alloc("recb%d" % i, (512,), F32) for i in range(2)]
            usts = [A.alloc("ustb%d" % i, (4, 512), BF16) for i in range(2)]
            dma("sp", KT_, KBT.ap.rearrange("(h p) t -> p h t", p=P), KBT.b, [KT_b])
            dma("sp", V_, VB.ap.rearrange("(t p) c -> p t c", p=P), VB.b, [V_b])
            blk = 0
            for g in range(2):
                for (q0, qn, is_lat) in q_blocks:
                    Qt, Qb = Qs[blk % 2]
                    SZ, SZb = SZs[blk % 2]
                    ust, ustb = usts[blk % 2]
                    blk += 1
                    r0 = g * 512
                    dma("sp", Qt[:, :, 0:qn], QBT.ap[r0:r0 + 512, q0:q0 + qn].rearrange("(r p) t -> p r t", p=P),
                        QBT.bufs(q0, qn), [Qb])
                    dma("sp", SZ[:, :, 0:qn], SZT.ap[512 + r0:512 + r0 + 512, q0:q0 + qn].rearrange("(r p) t -> p r t", p=P),
                        SZT.bufs(q0, qn), [SZb])
                    kts = list(range(NT)) if is_lat else [0, 1]
                    for qs in range(qn // P):
                        rhs = Qt[:, :, qs * P:(qs + 1) * P]
                        o_bank, s_bank = 4 + 2 * (qs % 2), 5 + 2 * (qs % 2)
                        rc = recs[qs % 2]
                        pend = []
                        nk = len(kts)

                        def _pv(item, lastf):
                            pki, pkt, pP, pPb = item
                            mm(ps[o_bank][:, :], V_[:, pkt, g * P:(g + 1) * P], pP, pki == 0, lastf, [V_b, pPb], [pb[o_bank]])
                            mm(ps[s_bank][:, :], ones_b, pP, pki == 0, lastf, [ones_b_b, pPb], [pb[s_bank]])

                        for ki, kt in enumerate(kts):
                            bk = rr["ps"] % 4
                            rr["ps"] += 1
                            mm(ps[bk][:, :], KT_[:, g, kt * P:(kt + 1) * P], rhs, True, True, [KT_b, Qb], [pb[bk]])
                            if len(pend) >= 2:
                                _pv(pend.pop(0), False)
                            Pq, Pqb = Pt[rr["ev"] % 4]
                            rr["ev"] += 1
                            act(Pq, ps[bk][:, :], AF.Exp, [pb[bk]], [Pqb], scale=sclB)
                            pend.append((ki, kt, Pq, Pqb))
                        while pend:
                            it_ = pend.pop(0)
                            _pv(it_, len(pend) == 0)
                        rec, recb = rc
                        S.op("dve", lambda e, o=rec, i=ps[s_bank][:, :]: e.reciprocal(out=o, in_=i), [pb[s_bank]], [recb])
                        tt("dve", rec, ps[o_bank][:, :], rec, ALU.mult, [pb[o_bank], recb], [recb])
                        tt("pool", ust[:, :, qs * P:(qs + 1) * P], rec.rearrange("p (r q) -> p r q", q=P),
                           SZ[:, :, qs * P:(qs + 1) * P], ALU.mult, [recb, SZb], [ustb])
                    dma("pool", UT.ap[512 + r0:512 + r0 + 512, q0:q0 + qn].rearrange("(r p) t -> p r t", p=P),
                        ust[:, :, 0:qn], [ustb], UT.bufs(q0, qn))
            A.release(mk)
            ckpt("attB")

            mk = A.mark()
            sclC = 64 ** -0.5
            KT_, KT_b = A.alloc("KTc", (4, T), BF16)
            V_, V_b = A.alloc("Vc", (NT, 512), BF16)
            Qs = [A.alloc("Qc%d" % i, (512,), BF16) for i in range(2)]
            SZs = [A.alloc("SZc%d" % i, (512,), BF16) for i in range(2)]
            Pt = [A.alloc("Pc%d" % i, (2, 512), BF16) for i in range(3)]
            r0t, r0b = A.alloc("rc0", (512,), F32)
            r1t, r1b = A.alloc("rc1", (512,), F32)
            o2t, o2b = A.alloc("oc2", (512,), F32)
            usts = [A.alloc("ustc%d" % i, (512,), BF16) for i in range(2)]
            dma("sp", KT_, KCT.ap.rearrange("(h p) t -> p h t", p=P), KCT.b, [KT_b])
            dma("sp", V_, VC.ap.rearrange("(t p) c -> p t c", p=P), VC.b, [V_b])
            blk = 0
            for h in range(4):
                for (q0, qn, is_lat) in q_blocks:
                    Qt, Qb = Qs[blk % 2]
                    SZ, SZb = SZs[blk % 2]
                    ust, ustb = usts[blk % 2]
                    blk += 1
                    dma("sp", Qt[:, 0:qn], QCT.ap[h * P:(h + 1) * P, q0:q0 + qn], QCT.bufs(q0, qn), [Qb])
                    dma("sp", SZ[:, 0:qn], SZT.ap[1536 + h * P:1536 + (h + 1) * P, q0:q0 + qn], SZT.bufs(q0, qn), [SZb])
                    kts = list(range(NT)) if is_lat else [0, 1]
                    O0, O1, S0, S1 = 4, 5, 6, 7
                    pend = None
                    for ki, kt in enumerate(kts):
                        b0 = (rr["ps"] % 2) * 2
                        rr["ps"] += 1
                        for m in range(2):
                            mm(ps[b0 + m][:, 0:qn], KT_[m * 64:(m + 1) * 64, h, kt * P:(kt + 1) * P], Qt[m * 64:(m + 1) * 64, 0:qn],
                               True, True, [KT_b, Qb], [pb[b0 + m]])
                        if pend is not None:
                            pki, pkt, pP, pPb = pend
                            for m, (ob_, sb_) in enumerate(((O0, S0), (O1, S1))):
                                mm(ps[ob_][:, 0:qn], V_[:, pkt, h * P:(h + 1) * P], pP[:, m, 0:qn], pki == 0, False, [V_b, pPb], [pb[ob_]])
                                mm(ps[sb_][:, 0:qn], ones_b, pP[:, m, 0:qn], pki == 0, False, [ones_b_b, pPb], [pb[sb_]])
                        Pq, Pqb = Pt[rr["ev"] % 3]
                        rr["ev"] += 1
                        for m in range(2):
                            act(Pq[:, m, 0:qn], ps[b0 + m][:, 0:qn], AF.Exp, [pb[b0 + m]], [Pqb], scale=sclC)
                        pend = (ki, kt, Pq, Pqb)
                    pki, pkt, pP, pPb = pend
                    for m, (ob_, sb_) in enumerate(((O0, S0), (O1, S1))):
                        mm(ps[ob_][:, 0:qn], V_[:, pkt, h * P:(h + 1) * P], pP[:, m, 0:qn], pki == 0, True, [V_b, pPb], [pb[ob_]])
                        mm(ps[sb_][:, 0:qn], ones_b, pP[:, m, 0:qn], pki == 0, True, [ones_b_b, pPb], [pb[sb_]])
                    S.op("dve", lambda e, o=r0t[:, 0:qn], i=ps[S0][:, 0:qn]: e.reciprocal(out=o, in_=i), [pb[S0]], [r0b])
                    S.op("dve", lambda e, o=r1t[:, 0:qn], i=ps[S1][:, 0:qn]: e.reciprocal(out=o, in_=i), [pb[S1]], [r1b])
                    tt("dve", r0t[:, 0:qn], ps[O0][:, 0:qn], r0t[:, 0:qn], ALU.mult, [pb[O0], r0b], [r0b])
                    tt("dve", r1t[:, 0:qn], ps[O1][:, 0:qn], r1t[:, 0:qn], ALU.mult, [pb[O1], r1b], [r1b])
                    stt(r0t[:, 0:qn], r1t[:, 0:qn], lamc[:, 0:1], r0t[:, 0:qn], ALU.mult, ALU.add, [r0b, r1b, lamc_b], [r0b])
                    act(o2t[:, 0:qn], r0t[:, 0:qn], AF.Square, [r0b], [o2b])
                    bq_ = (rr["ps"] % 2) * 2
                    rr["ps"] += 1
                    mm(ps[bq_][:, 0:qn], ones_f, o2t[:, 0:qn], True, True, [ones_f_b, o2b], [pb[bq_]])
                    ts("dve", r1t[:, 0:qn], ps[bq_][:, 0:qn], 1.0 / HD, ALU.mult, [pb[bq_]], [r1b], s2=EPS, op1=ALU.add)
                    act(r1t[:, 0:qn], r1t[:, 0:qn], AF.Ln, [r1b], [r1b])
                    act(r1t[:, 0:qn], r1t[:, 0:qn], AF.Exp, [r1b], [r1b], scale=-0.5)
                    stt(r0t[:, 0:qn], r0t[:, 0:qn], lamc[:, 1:2], r1t[:, 0:qn], ALU.mult, ALU.mult, [r0b, r1b, lamc_b], [r0b])
                    tt("dve", ust[:, 0:qn], r0t[:, 0:qn], SZ[:, 0:qn], ALU.mult, [r0b, SZb], [ustb])
                    dma("pool", UT.ap[1536 + h * P:1536 + (h + 1) * P, q0:q0 + qn], ust[:, 0:qn], [ustb], UT.bufs(q0, qn))
            A.release(mk)
            ckpt("attC")

            mk = A.mark()
            Ug, Ug_b = A.alloc("Ug", (KC, 1024), BF16)
            yT, _ = A.alloc("yT", (KC, 1024), BF16)
            yT_b = [A.extra_buf("yT_%d" % i) for i in range(8)]
            wbs = [A.alloc("wbo%d" % i, (KC, 512), BF16) for i in range(2)]
            wos = [A.alloc("wo%d" % i, (KC, 512), BF16) for i in range(2)]
            gbc = [A.alloc("gbc%d" % i, (D,), F32) for i in range(2)]
            gts = [A.alloc("gt%d" % i, (3, 512), BF16) for i in range(3)]
            m1s = [A.alloc("m1_%d" % i, (512,), F32) for i in range(2)]
            m2s = [A.alloc("m2_%d" % i, (512,), F32) for i in range(2)]
            m3s = [A.alloc("m3_%d" % i, (512,), F32) for i in range(2)]
            xrs = [A.alloc("xr%d" % i, (512,), F32) for i in range(3)]
            xos = [A.alloc("xo%d" % i, (512,), F32) for i in range(3)]
            dg, dg_b = A.alloc("dg", (P,), F32)
            for r in range(2):
                if r == 1 and not upd_ctx:
                    continue
                for kc in range(KC):
                    bk = rr["ps"] % 6
                    rr["ps"] += 1
                    ts("dve", dg, ident_f, modT[:, 32 + kc, r:r + 1], ALU.mult, [ident_f_b, modT_b], [dg_b])
                    mm(ps[bk][:, 0:P], ones_f, dg, True, True, [ones_f_b, dg_b], [pb[bk]])
                    cp("act", gbc[r][0][:, kc * P:(kc + 1) * P], ps[bk][:, 0:P], [pb[bk]], [gbc[r][1]])
            wbov = w_bo[li].rearrange("(kc p) n -> p kc n", p=P)
            wov = w_out[li].rearrange("(kc p) n -> p kc n", p=P)
            dgroups = [(256 + 1024 * i, 1024, 0) for i in range(4)]
            if upd_ctx:
                dgroups.append((0, 256, 1))
            Xnext = (xout if not is_final else Xs) if li == nl - 1 else Xs
            Xnext_b = (xo_b if not is_final else xs_b) if li == nl - 1 else xs_b
            wall = wbs + wos
            d_items = [(gi_, kind_, j_) for gi_ in range(len(dgroups)) for kind_ in ("bo", "wo") for j_ in range(4)]
            d_state = {"n": 0}

            def ensure_d(upto):
                while d_state["n"] <= min(upto, len(d_items) - 1):
                    i_ = d_state["n"]
                    w_t, w_b = wall[i_ % 4]
                    _, kind_, j_ = d_items[i_]
                    srcv = wbov if kind_ == "bo" else wov
                    dma("pool", w_t, srcv[:, :, j_ * 512:(j_ + 1) * 512], [], [w_b])
                    d_state["n"] += 1

            for dgi, (t0, gn, r) in enumerate(dgroups):
                ensure_d(dgi * 8 + 1)
                dma("sp", Ug[:, :, 0:gn], UT.ap[:, t0:t0 + gn].rearrange("(kc p) t -> p kc t", p=P), UT.bufs(t0, gn), [Ug_b])
                gmts = [(o, min(512, gn - o)) for o in range(0, gn, 512)]
                for fc4 in range(4):
                    didx = dgi * 8 + fc4
                    ensure_d(didx + 2)
                    wt_, wb_ = wall[didx % 4]
                    for sub in range(4):
                        fcn = fc4 * 4 + sub
                        for (mo, mn) in gmts:
                            gt, gtb = gts[rr["ev"] % 3]
                            m1, m1b = m1s[rr["ev"] % 2]
                            m2, m2b = m2s[rr["ev"] % 2]
                            m3, m3b = m3s[rr["ev"] % 2]
                            rr["ev"] += 1
                            dma("sp", gt[:, :, 0:mn],
                                GT.ap[:, t0 + mo:t0 + mo + mn].rearrange("(b f) t -> f b t", b=3)[fcn * P:(fcn + 1) * P],
                                GT.bufs(t0 + mo, mn), [gtb])
                            bks = []
                            for br, (k0, k1) in enumerate(((0, 4), (4, 12), (12, 16))):
                                bk = rr["ps"] % 6
                                rr["ps"] += 1
                                bks.append(bk)
                                for kc in range(k0, k1):
                                    mm(ps[bk][:, 0:mn], wt_[:, kc, sub * P:(sub + 1) * P], Ug[:, kc, mo:mo + mn],
                                       kc == k0, kc == k1 - 1, [wb_, Ug_b], [pb[bk]])
                            tt("dve", m1[:, 0:mn], ps[bks[0]][:, 0:mn], gt[:, 0, 0:mn], ALU.mult, [pb[bks[0]], gtb], [m1b])
                            tt("dve", m2[:, 0:mn], ps[bks[1]][:, 0:mn], gt[:, 1, 0:mn], ALU.mult, [pb[bks[1]], gtb], [m2b])
                            tt("dve", m3[:, 0:mn], ps[bks[2]][:, 0:mn], gt[:, 2, 0:mn], ALU.mult, [pb[bks[2]], gtb], [m3b])
                            tt("pool", m1[:, 0:mn], m1[:, 0:mn], m2[:, 0:mn], ALU.add, [m1b, m2b], [m1b])
                            ybs = [yT_b[(mo + i) // P] for i in range(0, mn, P)]
                            tt("pool", yT[:, fcn, mo:mo + mn], m1[:, 0:mn], m3[:, 0:mn], ALU.add, [m1b, m3b], ybs)
                for nc4 in range(4):
                    didx = dgi * 8 + 4 + nc4
                    ensure_d(didx + 2)
                    wt_, wb_ = wall[didx % 4]
                    for tsb in range(gn // P):
                        tg = t0 // P + tsb
                        xr, xrb = xrs[rr["ev"] % 3]
                        xo, xob = xos[rr["ev"] % 3]
                        rr["ev"] += 1
                        dma("sp", xr, Xcur[tg * P:(tg + 1) * P, nc4 * 512:(nc4 + 1) * 512], [Xcur_b[tg]], [xrb])
                        bk = rr["ps"] % 6
                        rr["ps"] += 1
                        for fc in range(KC):
                            mm(ps[bk][:, :], yT[:, fc, tsb * P:(tsb + 1) * P], wt_[:, fc, :], fc == 0, fc == KC - 1,
                               [wb_, yT_b[tsb]], [pb[bk]])
                        tt("dve", xo, ps[bk][:, :], gbc[r][0][:, nc4 * 512:(nc4 + 1) * 512], ALU.mult, [pb[bk], gbc[r][1]], [xob])
                        tt("pool", xo, xo, xr, ALU.add, [xob, xrb], [xob])
                        dma("pool", Xnext[tg * P:(tg + 1) * P, nc4 * 512:(nc4 + 1) * 512], xo, [xob], [Xnext_b[tg]])
            A.release(mk)
            if not upd_ctx and not is_final:
                pass
            Xcur, Xcur_b = Xnext, Xnext_b

        finals = []
        if is_final:
            mk = A.mark()
            fw, fw_b = A.alloc("fw", (D,), F32)
            xts = [A.alloc("fx%d" % i, (D,), F32) for i in range(2)]
            xns = [A.alloc("fn%d" % i, (D,), F32) for i in range(2)]
            junk, junk_b = A.alloc("fjunk", (D,), BF16)
            dma("sp", fw, fnw, [], [fw_b])
            for tl in range(L // P):
                tg = 2 + tl
                xt_, xtb_ = xts[tl % 2]
                xn_, xnb_ = xns[tl % 2]
                sst, sstb = stat[1 + tl % 2]
                dma("sp", xt_, Xcur[tg * P:(tg + 1) * P, :], [Xcur_b[tg]], [xtb_])
                act(junk, xt_, AF.Square, [xtb_], [junk_b, sstb], accum=sst[:, 0:1])
                ts("dve", sst[:, 1:2], sst[:, 0:1], 1.0 / D, ALU.mult, [sstb], [sstb], s2=EPS, op1=ALU.add)
                tt("pool", sst[:, 2:3], sst[:, 1:2], nhalf[:, 0:1], ALU.pow, [sstb, nhalf_b], [sstb])
                stt(xn_, xt_, sst[:, 2:3], fw, ALU.mult, ALU.mult, [xtb_, sstb, fw_b], [xnb_])
                finals.append(dma("pool", yout[tl * P:(tl + 1) * P, :], xn_, [xnb_], [yo_b[tl]]))
            A.release(mk)
        else:
            for b in xo_b:
                if b.w is not None:
                    finals.append(b.w)
        if S.dead:
            finals = S.dead_tail
        counts = S.finalize(final_waits=_dedupe([f for f in finals if f is not None]))
    return nc, counts


def _prep_shared(inputs, l0, nl):
    f = lambda a: np.ascontiguousarray(a, dtype=np.float32)
    sl = slice(l0, l0 + nl)
    colT = lambda a, n: f(a.reshape(a.shape[0], n, P).transpose(0, 2, 1))
    cosb, sinb = _rope_tables(HD)
    cosc, sinc = _rope_tables(64)
    sh = {
        "norm_wT": colT(inputs["norm_w"][sl], KC),
        "w_ada": f(inputs["w_ada"][sl]),
        "b_adaT": colT(inputs["b_ada"][sl], 48),
        "w_in": f(inputs["w_in"][sl]),
        "b_gateT": colT(inputs["b_gate"][sl], 48),
        "biasA": _bias_tables(f(inputs["rpb"][sl])),
        "qnw": f(np.broadcast_to(inputs["q_norm_w"][sl][:, None, :], (nl, P, HD))),
        "knw": f(np.broadcast_to(inputs["k_norm_w"][sl][:, None, :], (nl, P, HD))),
        "lamv": f(np.broadcast_to(np.stack([inputs["lam_q1"][sl], inputs["lam_k1"][sl], inputs["lam_q2"][sl],
                                            inputs["lam_k2"][sl]], axis=1)[:, None], (nl, P, 4, 64))),
        "sublnT": f(inputs["subln_w"][sl][:, :, None]),
        "w_bo": f(np.concatenate([inputs["w_bo_a"][sl], inputs["w_bo_b"][sl], inputs["w_bo_c"][sl]], axis=1)),
        "w_out": f(inputs["w_out"][sl]),
        "fnw": f(np.broadcast_to(inputs["final_norm_w"][None, :], (P, D))),
        "cosb": cosb, "sinb": sinb, "cosc": cosc, "sinc": sinc,
        "ident": np.eye(P, dtype=np.float32),
    }
    return sh


def _cc(inputs, b):
    cc = np.stack([inputs["c"][b], inputs["c_ctx"]], axis=0).astype(np.float32)
    return np.ascontiguousarray(cc.reshape(2, KC, P).transpose(2, 1, 0))


_PROG_CACHE = {}


def _get_prog(nl, first, is_final):
    key = (nl, first, is_final)
    if key not in _PROG_CACHE:
        _PROG_CACHE[key] = build_program(nl, first, is_final)[0]
    return _PROG_CACHE[key]


FUSED = True


def kernel(**inputs):
    inputs = {k: np.asarray(v) for k, v in inputs.items()}
    B = inputs["x"].shape[0]
    xs = [np.ascontiguousarray(np.concatenate([inputs["ctx"][b], inputs["x"][b]], axis=0), dtype=np.float32) for b in range(B)]
    ccs = [_cc(inputs, b) for b in range(B)]
    if FUSED:
        plan = [(0, DEPTH)]
    else:
        plan = [(l, 1) for l in range(DEPTH)]
    out = None
    for (l0, nl) in plan:
        is_final = (l0 + nl == DEPTH)
        nc = _get_prog(nl, l0, is_final)
        sh = _prep_shared(inputs, l0, nl)
        in_maps = []
        for b in range(B):
            m = dict(sh)
            m["xin"] = xs[b]
            m["ccT"] = ccs[b]
            in_maps.append(m)
        res = run_bass_kernel_spmd(nc, in_maps, core_ids=list(range(B)))
        if is_final:
            out = np.stack([np.asarray(res.results[b]["y"]) for b in range(B)], axis=0).astype(np.float32)
        else:
            xs = [np.ascontiguousarray(np.asarray(res.results[b]["xout"]), dtype=np.float32) for b in range(B)]
    return out
```
